# Optimizing a Trainium2 kernel written in Bass

```python
import jax, jax.numpy as jnp
from jax import lax
import numpy as np

D_MODEL = 2048
BATCH = 4
SEQ = 2048
DEPTH = 2
DEC_BATCH = 16
DEC_SEQ = 32
PAST_LEN = 4096

CHUNK = 64
D_MIX = D_MODEL
D_GROUP = D_MIX // 4
H_RET = 4
DK_RET = D_GROUP // H_RET
DV_RET = D_GROUP // H_RET
ROPE_BASE = 10000.0
H_LRU = 8
BLK_LRU = D_GROUP // H_LRU
LRU_C = 8.0
CONV_W = 4
H_SG = 4
SG_CHUNK = 128
H_DN = 4
DK_DN = D_GROUP // H_DN
DV_DN = D_GROUP // H_DN
D_IN = 12 * D_GROUP + 2 * H_DN
D_FF = 5632
ALPHA = (2.0 * DEPTH) ** 0.25
BETA_INIT = (8.0 * DEPTH) ** -0.25
EPS = 1e-5

kernel_name = 'hybrid_streaming_encoder_step'

F32 = jnp.float32


def layer_norm(x, g, b):
    xf = x.astype(F32)
    mu = jnp.mean(xf, -1, keepdims=True)
    var = jnp.mean(jnp.square(xf - mu), -1, keepdims=True)
    return ((xf - mu) * lax.rsqrt(var + EPS) * g + b).astype(x.dtype)


def swiglu(x, w_in, w_out):
    gate, up = jnp.split(x @ w_in, 2, axis=-1)
    return (jax.nn.silu(gate) * up) @ w_out


def causal_conv(x, buf, w):
    L = x.shape[1]
    xp = jnp.concatenate([buf.astype(x.dtype), x], axis=1)
    out = xp[:, 0:L] * w[0]
    for j in range(1, CONV_W):
        out = out + xp[:, j:j + L] * w[j]
    return out, xp[:, -(CONV_W - 1):]


def rotary(x, pos):
    half = x.shape[-1] // 2
    inv = ROPE_BASE ** (-jnp.arange(half, dtype=F32) / half)
    ang = pos.astype(F32)[:, None] * inv[None, :]
    cos = jnp.cos(ang)[None, :, None, :]
    sin = jnp.sin(ang)[None, :, None, :]
    x1, x2 = x[..., :half], x[..., half:]
    return jnp.concatenate([x1 * cos - x2 * sin, x1 * sin + x2 * cos], axis=-1)


def retention(q, k, v, state0):
    B, L, H, dk = q.shape
    c = min(CHUNK, L)
    n = L // c
    log_g = jnp.log1p(-jnp.exp2(-5.0 - jnp.arange(H, dtype=F32)))
    q = q.reshape(B, n, c, H, dk)
    k = k.reshape(B, n, c, H, dk) * (dk ** -0.5)
    v = v.astype(F32).reshape(B, n, c, H, -1)
    idx = jnp.arange(c, dtype=F32)
    diff = idx[:, None] - idx[None, :]
    dmask = jnp.where(diff >= 0, jnp.exp(log_g[:, None, None] * jnp.maximum(diff, 0.0)), 0.0)
    scores = jnp.einsum('bnihd,bnjhd->bnhij', q, k) * dmask
    o_intra = jnp.einsum('bnhij,bnjhe->bnihe', scores, v)
    k_dec = k * jnp.exp(log_g[None, :] * (c - 1.0 - idx)[:, None])[None, None, :, :, None]
    u = jnp.einsum('bnjhd,bnjhe->nbhde', k_dec, v)
    g_c = jnp.exp(log_g * c)[None, :, None, None]

    def step(s, u_n):
        return g_c * s + u_n, s

    s_final, s_starts = lax.scan(step, state0.astype(F32), u)
    q_dec = q * jnp.exp(log_g[None, :] * (idx + 1.0)[:, None])[None, None, :, :, None]
    o_inter = jnp.einsum('bnihd,nbhde->bnihe', q_dec, s_starts)
    o = (o_intra + o_inter).reshape(B, L, H, -1)
    return o, s_final.astype(state0.dtype)


def group_norm(o, g):
    mu = jnp.mean(o, -1, keepdims=True)
    var = jnp.mean(jnp.square(o - mu), -1, keepdims=True)
    on = (o - mu) * lax.rsqrt(var + EPS)
    return on.reshape(o.shape[0], o.shape[1], -1) * g


def rg_lru(x, w_a, b_a, w_x, b_x, lam, h0):
    x = x.astype(F32)
    B, L, _ = x.shape
    xb = x.reshape(B, L, H_LRU, BLK_LRU)
    r = jax.nn.sigmoid(jnp.einsum('blhi,hij->blhj', xb, w_a).reshape(B, L, -1) + b_a)
    i = jax.nn.sigmoid(jnp.einsum('blhi,hij->blhj', xb, w_x).reshape(B, L, -1) + b_x)
    log_a = -LRU_C * r * jax.nn.softplus(-lam.astype(F32))
    a = jnp.exp(log_a)
    b = jnp.sqrt(-jnp.expm1(2.0 * log_a)) * (i * x)

    def combine(e1, e2):
        a1, b1 = e1
        a2, b2 = e2
        return a1 * a2, a2 * b1 + b2

    a_cum, b_cum = lax.associative_scan(combine, (a, b), axis=1)
    h = a_cum * h0.astype(F32)[:, None, :] + b_cum
    return h, h[:, -1].astype(h0.dtype)


def spatial_gate(u, v, ln_g, ln_b, w_s, b_s):
    B, L, _ = v.shape
    vn = layer_norm(v, ln_g, ln_b)
    c = min(SG_CHUNK, L)
    n = L // c
    pos = jnp.arange(c)
    mask = (pos[None, :] // CHUNK) <= (pos[:, None] // CHUNK)
    w = jnp.where(mask[None], w_s[:, :c, :c], 0.0)
    vh = vn.reshape(B, n, c, H_SG, -1)
    s = jnp.einsum('hij,bnjhe->bnihe', w, vh) + b_s[:, :c].T[None, None, :, :, None]
    return u * s.reshape(B, L, -1), vn


def l2norm(x):
    xf = x.astype(F32)
    return xf * lax.rsqrt(jnp.sum(xf * xf, -1, keepdims=True) + 1e-6)


def gated_delta(q, k, v, g, beta, state0):
    B, L, H, dk = q.shape
    c = min(CHUNK, L)
    n = L // c

    def chunks(t):
        return t.astype(F32).reshape(B, n, c, H, -1).transpose(0, 3, 1, 2, 4)

    q = chunks(q) * (dk ** -0.5)
    k = chunks(k)
    v = chunks(v)
    g = g.reshape(B, n, c, H).transpose(0, 3, 1, 2)
    beta = beta.reshape(B, n, c, H).transpose(0, 3, 1, 2)
    gc = jnp.cumsum(g, axis=-1)
    tri = jnp.tril(jnp.ones((c, c), bool))
    strict = jnp.tril(jnp.ones((c, c), bool), -1)
    decay = jnp.exp(jnp.where(tri, gc[..., :, None] - gc[..., None, :], -jnp.inf))
    k_beta = k * beta[..., None]
    v_beta = v * beta[..., None]
    a_mat = jnp.where(strict, jnp.einsum('bhnid,bhnjd->bhnij', k_beta, k) * decay, 0.0)
    t_mat = a_mat + jnp.eye(c, dtype=F32)
    u = lax.linalg.triangular_solve(t_mat, v_beta, left_side=True, lower=True, unit_diagonal=True)
    w = lax.linalg.triangular_solve(t_mat, k_beta * jnp.exp(gc)[..., None], left_side=True, lower=True,
                                    unit_diagonal=True)
    attn = jnp.einsum('bhnid,bhnjd->bhnij', q, k) * decay
    q_dec = q * jnp.exp(gc)[..., None]
    g_last = gc[..., -1]
    k_dec = k * jnp.exp(g_last[..., None] - gc)[..., None]
    xs = (jnp.moveaxis(u, 2, 0), jnp.moveaxis(w, 2, 0), jnp.moveaxis(attn, 2, 0),
          jnp.moveaxis(q_dec, 2, 0), jnp.moveaxis(k_dec, 2, 0), jnp.moveaxis(g_last, 2, 0))

    def step(s, inp):
        u_n, w_n, attn_n, qd_n, kd_n, gl_n = inp
        v_new = u_n - jnp.einsum('bhid,bhde->bhie', w_n, s)
        o_n = jnp.einsum('bhid,bhde->bhie', qd_n, s) + jnp.einsum('bhij,bhje->bhie', attn_n, v_new)
        s = s * jnp.exp(gl_n)[..., None, None] + jnp.einsum('bhid,bhie->bhde', kd_n, v_new)
        return s, o_n

    s_final, o = lax.scan(step, state0.astype(F32), xs)
    o = o.transpose(1, 0, 3, 2, 4).reshape(B, L, H, -1)
    return o, s_final.astype(state0.dtype)


def gated_rms_norm(o, g, z):
    on = o * lax.rsqrt(jnp.mean(o * o, -1, keepdims=True) + EPS) * g
    return on.reshape(o.shape[0], o.shape[1], -1) * jax.nn.silu(z.astype(F32))


def split_columns(proj):
    sizes = (D_GROUP,) * 4 + (D_GROUP,) * 2 + (D_GROUP,) * 2 + (3 * D_GROUP, D_GROUP, H_DN, H_DN)
    idx = np.cumsum(sizes)[:-1].tolist()
    return jnp.split(proj, idx, axis=-1)


def token_mixers(h, pos, s_ret, s_lru, s_lru_conv, s_dn, s_dn_conv, p):
    B, L, _ = h.shape
    proj = h @ p['w_mix_in']
    a_q, a_k, a_v, a_g, b_y, b_x, c_u, c_v, d_qkv, d_z, d_a, d_b = split_columns(proj)
    q = rotary(a_q.astype(F32).reshape(B, L, H_RET, DK_RET), pos)
    k = rotary(a_k.astype(F32).reshape(B, L, H_RET, DK_RET), pos)
    o_ret, s_ret_new = retention(q, k, a_v.reshape(B, L, H_RET, DV_RET), s_ret)
    out_a = group_norm(o_ret, p['ret_norm_g']) * jax.nn.silu(a_g.astype(F32))
    xc, lru_conv_new = causal_conv(b_x, s_lru, p['lru_conv_w']) if False else causal_conv(b_x, s_lru_conv, p['lru_conv_w'])
    hs, lru_h_new = rg_lru(xc + p['lru_conv_b'], p['lru_w_a'], p['lru_b_a'], p['lru_w_x'], p['lru_b_x'],
                           p['lru_lam'], s_lru)
    out_b = hs * jax.nn.gelu(b_y.astype(F32))
    out_c, sg_v = spatial_gate(jax.nn.gelu(c_u), jax.nn.gelu(c_v), p['sg_ln_g'], p['sg_ln_b'],
                               p['sg_w'], p['sg_b'])
    qkv, dn_conv_new = causal_conv(d_qkv, s_dn_conv, p['dn_conv_w'])
    dq, dk, dv = jnp.split(jax.nn.silu(qkv), 3, axis=-1)
    dq = l2norm(dq.reshape(B, L, H_DN, DK_DN))
    dk = l2norm(dk.reshape(B, L, H_DN, DK_DN))
    g = -jnp.exp(p['dn_a_log'].astype(F32)) * jax.nn.softplus(d_a.astype(F32) + p['dn_dt_bias'])
    beta = jax.nn.sigmoid(d_b.astype(F32))
    o_dn, s_dn_new = gated_delta(dq, dk, dv.reshape(B, L, H_DN, DV_DN), g, beta, s_dn)
    out_d = gated_rms_norm(o_dn, p['dn_norm_g'], d_z)
    merged = jnp.concatenate([out_a, out_b, out_c.astype(F32), out_d], axis=-1).astype(h.dtype)
    return merged @ p['w_mix_out'], (s_ret_new, lru_h_new, lru_conv_new, s_dn_new, dn_conv_new, sg_v)


def run_trunk(x, pos, init_states, params):
    new_states = []
    for l in range(DEPTH):
        p = params[l]
        x = layer_norm(ALPHA * x + 0.5 * swiglu(x, p['ffn1_w_in'], p['ffn1_w_out']), p['ln1_g'], p['ln1_b'])
        mix, st = token_mixers(x, pos, *init_states[l], p)
        x = layer_norm(ALPHA * x + mix, p['ln2_g'], p['ln2_b'])
        x = layer_norm(ALPHA * x + 0.5 * swiglu(x, p['ffn2_w_in'], p['ffn2_w_out']), p['ln3_g'], p['ln3_b'])
        new_states.append(st)
    return x, new_states


def setup_inputs(seed: int = 0) -> dict:
    key = jax.random.key(seed)
    ks = iter(jax.random.split(key, 64))

    def nrm(shape, scale):
        return jax.random.normal(next(ks), shape, F32) * scale

    def unif(shape, lo, hi):
        return jax.random.uniform(next(ks), shape, F32, lo, hi)

    G = D_GROUP
    inp = {}
    inp['x_prompt'] = nrm((BATCH, SEQ, D_MODEL), 1.0)
    inp['x_sample'] = nrm((DEC_BATCH, DEC_SEQ, D_MODEL), 1.0)
    inp['state_ret'] = nrm((DEPTH, DEC_BATCH, H_RET, DK_RET, DV_RET), 1.0)
    inp['state_lru_h'] = nrm((DEPTH, DEC_BATCH, G), 0.5)
    inp['state_lru_conv'] = nrm((DEPTH, DEC_BATCH, CONV_W - 1, G), 1.0)
    inp['state_dn'] = nrm((DEPTH, DEC_BATCH, H_DN, DK_DN, DV_DN), 0.1)
    inp['state_dn_conv'] = nrm((DEPTH, DEC_BATCH, CONV_W - 1, 3 * G), 1.0)
    inp['ffn1_w_in'] = nrm((DEPTH, D_MODEL, 2 * D_FF), D_MODEL ** -0.5)
    inp['ffn1_w_out'] = nrm((DEPTH, D_FF, D_MODEL), BETA_INIT * D_FF ** -0.5)
    inp['ln1_g'] = 1.0 + nrm((DEPTH, D_MODEL), 0.02)
    inp['ln1_b'] = nrm((DEPTH, D_MODEL), 0.02)
    inp['w_mix_in'] = nrm((DEPTH, D_MODEL, D_IN), D_MODEL ** -0.5)
    inp['ret_norm_g'] = 1.0 + nrm((DEPTH, G), 0.02)
    inp['lru_conv_w'] = nrm((DEPTH, CONV_W, G), 0.5)
    inp['lru_conv_b'] = nrm((DEPTH, G), 0.02)
    inp['lru_w_a'] = nrm((DEPTH, H_LRU, BLK_LRU, BLK_LRU), BLK_LRU ** -0.5)
    inp['lru_b_a'] = nrm((DEPTH, G), 0.1)
    inp['lru_w_x'] = nrm((DEPTH, H_LRU, BLK_LRU, BLK_LRU), BLK_LRU ** -0.5)
    inp['lru_b_x'] = nrm((DEPTH, G), 0.1)
    a_c = unif((DEPTH, G), 0.9, 0.999)
    a_base = a_c ** (1.0 / LRU_C)
    inp['lru_lam'] = jnp.log(a_base) - jnp.log1p(-a_base)
    inp['sg_ln_g'] = 1.0 + nrm((DEPTH, G), 0.02)
    inp['sg_ln_b'] = nrm((DEPTH, G), 0.02)
    inp['sg_w'] = nrm((DEPTH, H_SG, SG_CHUNK, SG_CHUNK), 0.5 * SG_CHUNK ** -0.5)
    inp['sg_b'] = 1.0 + nrm((DEPTH, H_SG, SG_CHUNK), 0.1)
    inp['dn_conv_w'] = nrm((DEPTH, CONV_W, 3 * G), 0.5)
    inp['dn_a_log'] = jnp.log(unif((DEPTH, H_DN), 1.0, 16.0))
    dt = jnp.exp(unif((DEPTH, H_DN), float(np.log(1e-3)), float(np.log(1e-1))))
    inp['dn_dt_bias'] = dt + jnp.log(-jnp.expm1(-dt))
    inp['dn_norm_g'] = 1.0 + nrm((DEPTH, DV_DN), 0.02)
    inp['w_mix_out'] = nrm((DEPTH, D_MIX, D_MODEL), BETA_INIT * D_MIX ** -0.5)
    inp['ln2_g'] = 1.0 + nrm((DEPTH, D_MODEL), 0.02)
    inp['ln2_b'] = nrm((DEPTH, D_MODEL), 0.02)
    inp['ffn2_w_in'] = nrm((DEPTH, D_MODEL, 2 * D_FF), D_MODEL ** -0.5)
    inp['ffn2_w_out'] = nrm((DEPTH, D_FF, D_MODEL), BETA_INIT * D_FF ** -0.5)
    inp['ln3_g'] = 1.0 + nrm((DEPTH, D_MODEL), 0.02)
    inp['ln3_b'] = nrm((DEPTH, D_MODEL), 0.02)
    return inp


def reference(x_prompt, x_sample, state_ret, state_lru_h, state_lru_conv, state_dn, state_dn_conv,
              ffn1_w_in, ffn1_w_out, ln1_g, ln1_b, w_mix_in, ret_norm_g, lru_conv_w, lru_conv_b,
              lru_w_a, lru_b_a, lru_w_x, lru_b_x, lru_lam, sg_ln_g, sg_ln_b, sg_w, sg_b,
              dn_conv_w, dn_a_log, dn_dt_bias, dn_norm_g, w_mix_out, ln2_g, ln2_b,
              ffn2_w_in, ffn2_w_out, ln3_g, ln3_b):
    params = [dict(ffn1_w_in=ffn1_w_in[l], ffn1_w_out=ffn1_w_out[l], ln1_g=ln1_g[l], ln1_b=ln1_b[l],
                   w_mix_in=w_mix_in[l], ret_norm_g=ret_norm_g[l], lru_conv_w=lru_conv_w[l],
                   lru_conv_b=lru_conv_b[l], lru_w_a=lru_w_a[l], lru_b_a=lru_b_a[l], lru_w_x=lru_w_x[l],
                   lru_b_x=lru_b_x[l], lru_lam=lru_lam[l], sg_ln_g=sg_ln_g[l], sg_ln_b=sg_ln_b[l],
                   sg_w=sg_w[l], sg_b=sg_b[l], dn_conv_w=dn_conv_w[l], dn_a_log=dn_a_log[l],
                   dn_dt_bias=dn_dt_bias[l], dn_norm_g=dn_norm_g[l], w_mix_out=w_mix_out[l],
                   ln2_g=ln2_g[l], ln2_b=ln2_b[l], ffn2_w_in=ffn2_w_in[l], ffn2_w_out=ffn2_w_out[l],
                   ln3_g=ln3_g[l], ln3_b=ln3_b[l])
              for l in range(DEPTH)]

    bp, lp = x_prompt.shape[0], x_prompt.shape[1]
    dt_p = x_prompt.dtype
    prompt_init = [(jnp.zeros((bp, H_RET, DK_RET, DV_RET), dt_p), jnp.zeros((bp, D_GROUP), dt_p),
                    jnp.zeros((bp, CONV_W - 1, D_GROUP), dt_p), jnp.zeros((bp, H_DN, DK_DN, DV_DN), dt_p),
                    jnp.zeros((bp, CONV_W - 1, 3 * D_GROUP), dt_p)) for _ in range(DEPTH)]
    y_prompt, st_p = run_trunk(x_prompt, jnp.arange(lp), prompt_init, params)

    ls = x_sample.shape[1]
    sample_init = [(state_ret[l], state_lru_h[l], state_lru_conv[l], state_dn[l], state_dn_conv[l])
                   for l in range(DEPTH)]
    y_sample, st_s = run_trunk(x_sample, PAST_LEN + jnp.arange(ls), sample_init, params)

    ret_p = jnp.stack([s[0] for s in st_p])
    lru_h_p = jnp.stack([s[1] for s in st_p])
    lru_conv_p = jnp.stack([s[2] for s in st_p])
    dn_p = jnp.stack([s[3] for s in st_p])
    dn_conv_p = jnp.stack([s[4] for s in st_p])
    ret_s = jnp.stack([s[0] for s in st_s])
    lru_h_s = jnp.stack([s[1] for s in st_s])
    lru_conv_s = jnp.stack([s[2] for s in st_s])
    dn_s = jnp.stack([s[3] for s in st_s])
    dn_conv_s = jnp.stack([s[4] for s in st_s])
    sg_v_s = jnp.stack([s[5] for s in st_s])
    return (y_prompt, y_sample, ret_p, lru_h_p, lru_conv_p, dn_p, dn_conv_p,
            ret_s, lru_h_s, lru_conv_s, dn_s, dn_conv_s, sg_v_s)
```

```python
import contextlib
import math
import numpy as np
import concourse.bass as bass
import concourse.mybir as mybir
from concourse.bass_utils import run_bass_kernel_spmd

F32 = mybir.dt.float32
BF16 = mybir.dt.bfloat16
ALU = mybir.AluOpType
AF = mybir.ActivationFunctionType
AX = mybir.AxisListType

D = 2048
DFF = 5632
G = 512
DIN = 6152
NL = 2
KC = 16
HC = 44
ALPHA = (2.0 * NL) ** 0.25
EPS = 1e-5
PAST = 4096
N_DMA_SEMS = 12


class Prog:
    ENGS = ["pe", "act", "dve", "pool", "sp"]

    def __init__(self, nc):
        self.nc = nc
        self.ops = []
        self.last_w = {}
        self.readers = {}

    def add(self, eng, fn, r=(), w=(), dma=False, extra=()):
        idx = len(self.ops)
        r = list(r)
        w = list(w)
        if any(isinstance(k, str) and k.startswith("o:") for k in r + w):
            r.append("OVL")
        deps = {}
        for k in r:
            if k in self.last_w:
                deps[self.last_w[k]] = True
            if isinstance(k, str) and k[0] == "p" and k[1:2].isdigit() or k in ("pS1", "pS2"):
                for q in self.readers.get(k, ()):
                    deps.setdefault(q, False)
        for k in w:
            if k in self.last_w:
                deps.setdefault(self.last_w[k], False)
            for q in self.readers.get(k, ()):
                deps.setdefault(q, False)
        for q in extra:
            deps[q] = True
        for k in r:
            self.readers.setdefault(k, []).append(idx)
        for k in w:
            self.last_w[k] = idx
            self.readers[k] = []
        deps.pop(idx, None)
        self.ops.append(dict(eng=eng, fn=fn, deps=deps, dma=dma, idx=idx))
        return idx

    def _needs_sync(self, o, D, raw):
        if D["dma"] or o["dma"]:
            return True
        if D["eng"] != o["eng"]:
            return True
        if o["eng"] == "pe":
            return False
        return raw

    def emit(self):
        nc = self.nc
        ops = self.ops
        need_inc = [False] * len(ops)
        for o in ops:
            best = {}
            nd = {}
            for d, raw in o["deps"].items():
                Dd = ops[d]
                if Dd["dma"]:
                    nd[d] = raw
                    continue
                if self._needs_sync(o, Dd, raw):
                    if best.get(Dd["eng"], -1) < d:
                        best[Dd["eng"]] = d
            for e, d in best.items():
                nd[d] = True
                need_inc[d] = True
            o["deps"] = nd
        cnt = {e: 0 for e in self.ENGS}
        incval = [0] * len(ops)
        for o in ops:
            if o["dma"]:
                continue
            if need_inc[o["idx"]]:
                cnt[o["eng"]] += 1
            incval[o["idx"]] = cnt[o["eng"]]
        dma_count = {e: 0 for e in self.ENGS}
        dma_sem = {}
        for o in ops:
            if o["dma"]:
                q = o["eng"]
                j = dma_count[q]
                dma_count[q] += 1
                dma_sem[o["idx"]] = (q, j % N_DMA_SEMS, 16 * (j // N_DMA_SEMS + 1))
        with contextlib.ExitStack() as es:
            esem = {e: es.enter_context(nc.semaphore("s_" + e)) for e in ["pe", "act", "dve", "pool"]}
            dsem = {}
            for q in self.ENGS:
                if dma_count[q] > 0:
                    dsem[q] = [es.enter_context(nc.semaphore("d_%s_%d" % (q, i)))
                               for i in range(min(N_DMA_SEMS, dma_count[q]))]
            block = es.enter_context(nc.Block())
            per_eng = {e: [o for o in ops if o["eng"] == e] for e in self.ENGS}

            def run_engine(ename, eng):
                waited = {}

                def wait(sem, val):
                    if waited.get(sem.num, 0) >= val:
                        return
                    waited[sem.num] = val
                    eng.wait_ge(sem, val)

                for o in per_eng[ename]:
                    for d in sorted(o["deps"]):
                        Dd = ops[d]
                        if Dd["dma"]:
                            q, slot, val = dma_sem[d]
                            wait(dsem[q][slot], val)
                        else:
                            wait(esem[Dd["eng"]], incval[d])
                    if o["dma"]:
                        q, slot, val = dma_sem[o["idx"]]
                        if val > 16:
                            wait(dsem[q][slot], val - 16)
                        o["fn"](eng).then_inc(dsem[q][slot], 16)
                    else:
                        ins = o["fn"](eng)
                        if need_inc[o["idx"]]:
                            ins.then_inc(esem[ename], 1)
                if ename == "sp":
                    for q in dsem:
                        n = dma_count[q]
                        for slot in range(len(dsem[q])):
                            uses = (n - slot + N_DMA_SEMS - 1) // N_DMA_SEMS
                            if uses > 0:
                                eng.wait_ge(dsem[q][slot], 16 * uses)

            @block.sync
            def _(e):
                run_engine("sp", e)

            @block.tensor
            def _(e):
                run_engine("pe", e)

            @block.scalar
            def _(e):
                run_engine("act", e)

            @block.vector
            def _(e):
                run_engine("dve", e)

            @block.gpsimd
            def _(e):
                run_engine("pool", e)


def host_consts():
    c = {}
    c["c_ident"] = np.eye(128, dtype=np.float32)
    half = 64
    inv = (10000.0 ** (-np.arange(half, dtype=np.float32) / half)).astype(np.float32)
    pos = np.concatenate([np.arange(2048), PAST + np.arange(32)]).astype(np.float32)
    ang = (pos[None, :] * inv[:, None]).astype(np.float32)
    cos = np.cos(ang).astype(np.float32)
    sin = np.sin(ang).astype(np.float32)
    c["c_cos"] = np.ascontiguousarray(np.concatenate([cos, cos], 0))
    c["c_sin"] = np.ascontiguousarray(np.concatenate([sin, -sin], 0))
    log_g = np.log1p(-np.exp2(-5.0 - np.arange(4, dtype=np.float64)))
    gam = {}
    rt = np.zeros((128, 2, 2, 4, 128), np.float32)
    kd = np.zeros((128, 2, 4), np.float32)
    for ki, cc in enumerate((128, 32)):
        idx = np.arange(cc, dtype=np.float64)
        diff = idx[None, :] - idx[:, None]
        dmT = np.where(diff >= 0, np.exp(log_g[:, None, None] * np.maximum(diff, 0.0)), 0.0) * (128 ** -0.5)
        rt[:cc, ki, 0, :, :cc] = dmT.transpose(1, 0, 2)
        qd = np.exp(log_g[:, None] * (idx + 1.0)[None, :])
        rt[:, ki, 1, :, :cc] = qd[None]
        kd[:cc, ki, :] = np.exp(log_g[None, :] * (cc - 1.0 - idx)[:, None]) * (128 ** -0.5)
        gam[ki] = [float(np.exp(log_g[h] * cc)) for h in range(4)]
    c["c_rt"] = np.ascontiguousarray(rt.reshape(128, 16, 128))
    c["c_kd"] = kd
    c["gam"] = gam
    m = np.zeros((128, 3, 128), np.float32)
    p = np.arange(128)
    m[:, 0, :] = (p[:, None] // 64 <= p[None, :] // 64)
    i64 = np.arange(64)
    m[:64, 1, :64] = (i64[:, None] > i64[None, :])
    m[:64, 2, :64] = (i64[None, :] >= i64[:, None])
    c["c_mask"] = m
    return c


NFM = 184
NROW = 1536


class Builder:
    def __init__(self, cfg):
        self.cfg = cfg
        self.nt = cfg.get("ntiles", 4)
        self.depth = cfg.get("depth", NL)
        self.stage = cfg.get("stage", 99)
        self.groups = cfg.get("groups", "ABCD")
        self.LP = max(self.nt, 1) * 512
        self.hc = host_consts()

    def MM(self, out, lhsT, rhs, start, stop, r, w):
        self.P.add("pe", lambda e: e.matmul(out, lhsT, rhs, start=start, stop=stop), r=r, w=w)

    def ACT(self, out, in_, func, r, w, bias=None, scale=None):
        kw = {}
        if bias is not None:
            kw["bias"] = bias
        if scale is not None:
            kw["scale"] = scale
        self.P.add("act", lambda e: e.activation(out=out, in_=in_, func=func, **kw), r=r, w=w)

    def TS(self, out, in0, s1, s2, op0, op1, r, w, eng="dve"):
        if s2 is None:
            self.P.add(eng, lambda e: e.tensor_scalar(out=out, in0=in0, scalar1=s1, scalar2=None, op0=op0), r=r, w=w)
        else:
            self.P.add(eng, lambda e: e.tensor_scalar(out=out, in0=in0, scalar1=s1, scalar2=s2, op0=op0, op1=op1), r=r, w=w)

    def TT(self, out, in0, in1, op, r, w, eng="dve"):
        self.P.add(eng, lambda e: e.tensor_tensor(out=out, in0=in0, in1=in1, op=op), r=r, w=w)

    def STT(self, out, in0, scalar, in1, op0, op1, r, w, eng="dve"):
        self.P.add(eng, lambda e: e.scalar_tensor_tensor(out=out, in0=in0, scalar=scalar, in1=in1, op0=op0, op1=op1), r=r, w=w)

    def CP(self, out, in_, r, w, eng="dve"):
        if eng == "act":
            self.P.add("act", lambda e: e.activation(out=out, in_=in_, func=AF.Copy), r=r, w=w)
        else:
            self.P.add(eng, lambda e: e.tensor_copy(out=out, in_=in_), r=r, w=w)

    def RECIP(self, out, in_, r, w):
        self.P.add("dve", lambda e: e.reciprocal(out=out, in_=in_), r=r, w=w)

    def RSUM(self, out, in_, r, w):
        self.P.add("dve", lambda e: e.reduce_sum(out=out, in_=in_, axis=AX.X), r=r, w=w)

    def SCAN(self, out, d0, d1, init, r, w):
        self.P.add("dve", lambda e: e.tensor_tensor_scan(out=out, data0=d0, data1=d1, initial=init, op0=ALU.mult, op1=ALU.add), r=r, w=w)

    def rsqrt(self, out, in_, eps, r, w, tmp, tmpk, scale=1.0):
        self.TS(tmp, in_, scale, eps, ALU.mult, ALU.add, r=r, w=[tmpk])
        self.ACT(tmp, tmp, AF.Sqrt, r=[tmpk], w=[tmpk])
        self.RECIP(out, tmp, r=[tmpk], w=w)

    def MEMSET(self, ap, val, w, eng="dve"):
        self.P.add(eng, lambda e: e.memset(ap, val), w=w)

    def DMA(self, q, out, in_, r, w, extra=()):
        return self.P.add(q, lambda e: e.dma_start(out=out, in_=in_), r=r, w=w, dma=True, extra=extra)

    def ps(self):
        i = self._ps_i
        self._ps_i = (i + 1) % self.NRR
        return self.psb[i], "p%d" % i

    def barrier(self):
        self.P.add("dve", lambda e: e.memset(self.bar[:], 0.0), w=["OVL", "bar"])

    def mt_reset(self, base=16384):
        self._mt = base
        self._mtn = getattr(self, "_mtn", 0)

    def mt(self, shape, dt, name=None):
        n = 1
        for d in shape[1:]:
            n *= d
        esz = 4 if dt == F32 else 2
        off = (self._mt + 31) // 32 * 32
        self._mt = off + n * esz
        assert self._mt <= self.OVL_BYTES, ("overlay overflow", self._mt)
        base = self.ovl32 if dt == F32 else self.ovl
        ap = base[0:shape[0], off // esz: off // esz + n]
        if len(shape) == 3:
            ap = ap.rearrange("p (a b) -> p a b", a=shape[1])
        elif len(shape) == 4:
            ap = ap.rearrange("p (a b c) -> p a b c", a=shape[1], b=shape[2])
        self._mtn += 1
        return ap, "o:t%d" % self._mtn

    def build(self):
        nc = bass.Bass("TRN2", target_bir_lowering=False)
        self.nc = nc
        self.P = Prog(nc)
        LP = self.LP

        def din(name, shape):
            return nc.dram_tensor(name, list(shape), F32, kind="ExternalInput").ap()

        def dout(name, shape):
            return nc.dram_tensor(name, list(shape), F32, kind="ExternalOutput").ap()

        self.xp = din("xp", [LP, D])
        self.xs = din("xs", [64, D])
        self.W = {}
        wshapes = dict(w1i=[D, 2 * DFF], w1o=[DFF, D], wmi=[D, DIN], wmo=[D, D], w2i=[D, 2 * DFF], w2o=[DFF, D])
        self.wshapes = wshapes
        for nm, sh in wshapes.items():
            self.W[nm] = din(nm, [NL] + sh)
        self.S = {nm: nc.dram_tensor("s_" + nm, [NL] + sh, BF16, kind="Internal").ap() for nm, sh in wshapes.items()}
        self.pp_fm = din("pp_fm", [NL, 128, NFM])
        self.pp_row = din("pp_row", [NL, 128, NROW])
        self.pp_bd = din("pp_bd", [NL, 128, 8, 128])
        self.pp_sgw = din("pp_sgw", [NL, 128, 4, 128])
        self.pp_sgws = din("pp_sgws", [NL, 32, 4, 32])
        self.pp_dn4 = din("pp_dn4", [NL, 4, 2])
        self.c_ident = din("c_ident", [128, 128])
        self.c_cos = din("c_cos", [128, 2080])
        self.c_sin = din("c_sin", [128, 2080])
        self.c_rt = din("c_rt", [128, 16, 128])
        self.c_kd = din("c_kd", [128, 2, 4])
        self.c_mask = din("c_mask", [128, 3, 128])
        self.st_ret = din("st_ret", [NL, 2, 4, 128, 128])
        self.st_dn = din("st_dn", [NL, 2, 4, 128, 128])
        self.st_lruh = din("st_lruh", [NL, 2, 128, 4])
        self.st_lruconv = din("st_lruconv", [NL, 2, 128, 4, 3])
        self.st_dnconv = din("st_dnconv", [NL, 2, 128, 12, 3])
        self.yp = dout("yp", [LP, D])
        self.ys = dout("ys", [64, D])
        self.o_ret = dout("o_ret", [NL, 3, 4, 128, 128])
        self.o_dn = dout("o_dn", [NL, 3, 4, 128, 128])
        self.o_lruh = dout("o_lruh", [NL, 3, 128, 4])
        self.o_lruconv = dout("o_lruconv", [NL, 3, 128, 4, 3])
        self.o_dnconv = dout("o_dnconv", [NL, 3, 128, 12, 3])
        self.o_sgv = dout("o_sgv", [NL, 2, 32, 512])
        if self.cfg.get("dbg"):
            self.dbg_mg = dout("dbg_mg", [128, KC, 512])
            self.dbg_q = [nc.dram_tensor("dbg_q%d" % i, [128, 12, 512], BF16, kind="ExternalOutput").ap() for i in range(2)]

        with contextlib.ExitStack() as es:
            self.es = es

            def sb(name, shape, dt):
                return es.enter_context(nc.sbuf_tensor(name, list(shape), dt))

            self.sb = sb
            T = 512
            self.xres = sb("xres", [128, KC, T], F32)
            self.xbf = sb("xbf", [128, KC, T], BF16)
            self.wring = [sb("wr%d" % i, [128, KC, 128], BF16) for i in range(4)]
            self.woring = [sb("wo%d" % i, [128, HC, 128], BF16) for i in range(2)]
            self.OVL_BYTES = 61440
            self.ovl = sb("ovl", [128, self.OVL_BYTES // 2], BF16)
            self.ovl32 = self.ovl.bitcast(F32)
            cb = 45056

            def c32(i):
                return self.ovl32[:, (cb + i * 2048) // 4:(cb + (i + 1) * 2048) // 4]

            def c16(i):
                return self.ovl[:, (cb + 8192 + i * 1024) // 2:(cb + 8192 + (i + 1) * 1024) // 2]

            self.tA = [c32(0), c32(1)]
            self.sil = [c32(2), c32(3)]
            self.ybf = [c16(0), c16(1)]
            self.ysq = [c16(2), c16(3)]
            self.MT_TOP_A = 45056
            self.lnm = sb("lnm", [128, T], F32)
            self.lnr = sb("lnr", [128, T], F32)
            self.lnn = sb("lnn", [128, T], F32)
            self.ident32 = sb("ident32", [128, 128], F32)
            self.identb = sb("identb", [128, 128], BF16)
            self.onesb = sb("onesb", [128, 128], BF16)
            self.ones4 = sb("ones4", [4, 128], F32)
            self.bar = sb("bar", [128, 1], F32)
            self.ppfm = sb("ppfm", [128, NL, NFM], F32)
            self.pprow = sb("pprow", [128, NROW], F32)
            self.bd32 = None
            self.bdb = sb("bdb", [128, NL * 8, 128], BF16)
            self.sgwb = sb("sgwb", [128, NL, 4, 128], BF16)
            self.sgwsb = sb("sgwsb", [32, NL, 4, 32], BF16)
            self.dn4 = sb("dn4", [4, NL, 4], F32)
            self.lrup = sb("lrup", [128, NL, 4, 2], F32)
            self.cosb = sb("cosb", [128, T], F32)
            self.sinb = sb("sinb", [128, T], F32)
            self.rt = sb("rt", [128, 16, 128], F32)
            self.kd = sb("kd", [128, 2, 4], F32)
            self.mask = sb("mask", [128, 3, 128], F32)
            self.Sret = sb("Sret", [128, NL * 2, 512], F32)
            self.Sdn = sb("Sdn", [128, NL * 2, 512], F32)
            self.hst = sb("hst", [128, NL * 2, 4], F32)
            self.lct = sb("lct", [128, NL * 2 * 4, 3], F32)
            self.dct = sb("dct", [128, NL * 2 * 12, 3], F32)
            self.psb = [es.enter_context(nc.psum_tensor("pb%d" % i, [128, 512], F32)) for i in range(8)]
            self.NRR = 6
            self._ps_i = 0
            self.pS1, self.pS2 = self.psb[6], self.psb[7]
            print("sbuf bytes remaining", nc.sbuf_bytes_remaining)

            self.setup()
            self.casts()
            for ti in range(self.nt):
                self.run_tile("p", ti)
            if self.nt > 0:
                self.store_states("p")
            if self.cfg.get("sample", True):
                self.load_states_sample()
                self.run_tile("s", 0)
                self.store_states("s")
            self.P.emit()
        return nc

    def setup(self):
        self.DMA("sp", self.ident32[:], self.c_ident, r=[], w=["ident32"])
        self.CP(self.identb[:], self.ident32[:], r=["ident32"], w=["identb"])
        self.MEMSET(self.onesb[:], 1.0, w=["onesb"])
        self.MEMSET(self.ones4[:], 1.0, w=["ones4"])
        for l in range(NL):
            self.DMA("sp", self.ppfm[:, l, :], self.pp_fm[l], r=[], w=["ppfm"])
        self.DMA("sp", self.rt[:], self.c_rt, r=[], w=["rt"])
        self.DMA("sp", self.kd[:], self.c_kd, r=[], w=["kd"])
        self.DMA("sp", self.mask[:], self.c_mask, r=[], w=["mask"])
        for l in range(NL):
            self.DMA("sp", self.dn4[:, l, 0:2], self.pp_dn4[l], r=[], w=["dn4"])
        self.ACT(self.dn4[:, :, 2], self.dn4[:, :, 0], AF.Exp, r=["dn4"], w=["dn4"])
        self.TS(self.dn4[:, :, 2], self.dn4[:, :, 2], -1.0, None, ALU.mult, None, r=["dn4"], w=["dn4"])
        self.barrier()
        self.mt_reset(0)
        for l in range(NL):
            t, tk = self.mt([128, 8, 128], F32)
            self.DMA("sp", t, self.pp_bd[l], r=[], w=[tk])
            self.CP(self.bdb[:, l * 8:(l + 1) * 8, :], t, r=[tk], w=["bdb"])
            t2, t2k = self.mt([128, 4, 128], F32)
            self.DMA("sp", t2, self.pp_sgw[l], r=[], w=[t2k])
            for h in range(4):
                self.TT(self.sgwb[:, l, h, :], t2[:, h, :], self.mask[:, 0, :], ALU.mult, r=[t2k, "mask"], w=["sgwb"])
            t3, t3k = self.mt([32, 4, 32], F32)
            self.DMA("sp", t3, self.pp_sgws[l], r=[], w=[t3k])
            self.CP(self.sgwsb[:, l], t3, r=[t3k], w=["sgwsb"])
        for l in range(NL):
            lam = self.ppfm[:, l, 124:128]
            t, tk = self.mt([128, 4], F32)
            self.ACT(t, lam, AF.Exp, r=["ppfm"], w=[tk], scale=-1.0)
            self.ACT(t, t, AF.Ln, r=[tk], w=[tk], bias=1.0)
            self.TS(self.lrup[:, l, :, 0], t, -8.0, None, ALU.mult, None, r=[tk], w=["lrup"])
            self.TS(self.lrup[:, l, :, 1], t, -16.0, None, ALU.mult, None, r=[tk], w=["lrup"])
        for nm in ["Sret", "Sdn", "hst", "lct", "dct"]:
            self.MEMSET(getattr(self, nm)[:], 0.0, w=[nm])

    CBLK = dict(w1i=1408, w2i=1408, w1o=512, w2o=512, wmi=1024, wmo=512)

    def casts(self):
        self.cast_ops = {}
        order = ["w1i", "w1o", "wmi", "wmo", "w2i", "w2o"]
        for l in range(self.depth):
            for nm in order:
                R, C = self.wshapes[nm]
                cb = self.CBLK[nm]
                nblk = (C + cb - 1) // cb
                if nm in ("w1i", "w2i"):
                    blks = [0, 4, 1, 5, 2, 6, 3, 7]
                else:
                    blks = list(range(nblk))
                for bi in blks:
                    ids = []
                    c0, c1 = bi * cb, min(C, (bi + 1) * cb)
                    if not self.cfg.get("nocast"):
                        for rb in range(R // 128):
                            ids.append(self.DMA("pool", self.S[nm][l, rb * 128:(rb + 1) * 128, c0:c1],
                                                self.W[nm][l, rb * 128:(rb + 1) * 128, c0:c1], r=[], w=[]))
                    self.cast_ops[(nm, l, bi)] = ids

    def cast_deps(self, nm, l, col0, ncols):
        cb = self.CBLK[nm]
        out = []
        for bi in range(col0 // cb, (col0 + ncols - 1) // cb + 1):
            out += self.cast_ops[(nm, l, bi)]
        return out

    def load_panel(self, nm, l, col0, ncols, slot):
        src = self.S[nm][l, :, col0:col0 + ncols].rearrange("(kc p) n -> p kc n", p=128)
        self.DMA("sp", self.wring[slot][:, :, 0:ncols], src, r=[], w=["wr%d" % slot], extra=self.cast_deps(nm, l, col0, ncols))

    def load_wo(self, nm, l, m, slot):
        src = self.S[nm][l, :, m * 128:(m + 1) * 128].rearrange("(fc p) n -> p fc n", p=128)
        self.DMA("sp", self.woring[slot][:], src, r=[], w=["wo%d" % slot], extra=self.cast_deps(nm, l, m * 128, 128))

    def pj_begin(self, order):
        self.pj_order = order
        self.pj_issued = 0
        self.pj_pos = -1

    def pj_next(self):
        self.pj_pos += 1
        while self.pj_issued < len(self.pj_order) and self.pj_issued <= self.pj_pos + 2:
            nm, l, col0, ncols = self.pj_order[self.pj_issued]
            self.load_panel(nm, l, col0, ncols, self.pj_issued % 4)
            self.pj_issued += 1
        return self.pj_pos % 4

    def proj(self, ncols=128):
        T = self.T
        slot = self.pj_next()
        pb, pk = self.ps()
        for kc in range(KC):
            self.MM(pb[0:ncols, 0:T], self.wring[slot][:, kc, 0:ncols], self.xbf[:, kc, 0:T], kc == 0, kc == KC - 1,
                    r=["wr%d" % slot, "xbf%d" % kc], w=[pk])
        return pb, pk

    def run_tile(self, kind, ti):
        T = 512 if kind == "p" else 64
        self.T = T
        self.kind = kind
        self.ki = 0 if kind == "p" else 1
        self.ti = ti
        self.segs = [(0, 512, 0)] if kind == "p" else [(0, 32, 0), (32, 32, 1)]
        if not self.cfg.get("noload"):
            self.load_x(kind, ti)
        if kind == "p":
            self.DMA("sp", self.cosb[:, 0:T], self.c_cos[:, ti * 512:(ti + 1) * 512], r=[], w=["cosb"])
            self.DMA("sp", self.sinb[:, 0:T], self.c_sin[:, ti * 512:(ti + 1) * 512], r=[], w=["sinb"])
        else:
            for s in range(2):
                self.DMA("sp", self.cosb[:, s * 32:(s + 1) * 32], self.c_cos[:, 2048:2080], r=[], w=["cosb"])
                self.DMA("sp", self.sinb[:, s * 32:(s + 1) * 32], self.c_sin[:, 2048:2080], r=[], w=["sinb"])
        for l in range(self.depth):
            if self.stage >= 1:
                self.ffn(l, "w1i", "w1o", 0)
            if self.stage >= 2:
                self.mixer(l)
            if self.stage >= 3:
                self.ffn(l, "w2i", "w2o", 4)
        if not self.cfg.get("nostore"):
            self.store_x(kind, ti)

    def load_x(self, kind, ti):
        T = self.T
        nb = (T + 127) // 128
        self.barrier()
        for b in range(nb):
            rows = min(128, T - b * 128)
            io = self.ovl32[0:rows, b % 2 * D:(b % 2 + 1) * D]
            iok = "o:io%d" % (b % 2)
            src = (self.xp[ti * 512 + b * 128: ti * 512 + b * 128 + rows, :] if kind == "p" else self.xs[0:rows, :])
            self.DMA("sp", io, src, r=[], w=[iok])
            for g in range(4):
                pb, pk = self.ps()
                for q in range(4):
                    m = g * 4 + q
                    self.MM(pb[:, q * 128:q * 128 + rows], io[:, m * 128:(m + 1) * 128], self.ident32[0:rows, 0:rows],
                            True, True, r=[iok, "ident32"], w=[pk])
                pv = pb[:].rearrange("p (q t) -> p q t", q=4)[:, :, 0:rows]
                self.CP(self.xres[:, g * 4:(g + 1) * 4, b * 128:b * 128 + rows], pv, r=[pk],
                        w=["xres%d" % (g * 4 + q) for q in range(4)])
                for q in range(4):
                    self.CP(self.xbf[:, g * 4 + q, b * 128:b * 128 + rows], self.xres[:, g * 4 + q, b * 128:b * 128 + rows],
                            r=["xres%d" % (g * 4 + q)], w=["xbf%d" % (g * 4 + q)], eng="act")

    def store_x(self, kind, ti):
        T = self.T
        nb = (T + 127) // 128
        self.barrier()
        for b in range(nb):
            rows = min(128, T - b * 128)
            io = self.ovl32[0:rows, b % 2 * D:(b % 2 + 1) * D]
            iok = "o:io%d" % (b % 2)
            for g in range(4):
                pb, pk = self.ps()
                for q in range(4):
                    m = g * 4 + q
                    self.MM(pb[0:rows, q * 128:(q + 1) * 128], self.xres[:, m, b * 128:b * 128 + rows], self.ident32[:],
                            True, True, r=["xres%d" % m, "ident32"], w=[pk])
                self.CP(io[:, g * 512:(g + 1) * 512], pb[0:rows, :], r=[pk], w=[iok + "_%d" % g], eng=("act" if g % 2 else "dve"))
            dst = (self.yp[ti * 512 + b * 128: ti * 512 + b * 128 + rows, :] if kind == "p" else self.ys[0:rows, :])
            self.DMA("sp", dst, io, r=[iok + "_%d" % g for g in range(4)], w=[iok + "_%d" % g for g in range(4)] + [iok])

    def load_states_sample(self):
        for l in range(NL):
            for s in range(2):
                q = l * 2 + s
                self.DMA("sp", self.Sret[:, q, :].rearrange("p (h e) -> p h e", h=4), self.st_ret[l, s].rearrange("h k v -> k h v"), r=[], w=["Sret"])
                self.DMA("sp", self.Sdn[:, q, :].rearrange("p (h e) -> p h e", h=4), self.st_dn[l, s].rearrange("h k v -> k h v"), r=[], w=["Sdn"])
                self.DMA("sp", self.hst[:, q, :], self.st_lruh[l, s], r=[], w=["hst"])
                self.DMA("sp", self.lct[:, q * 4:(q + 1) * 4, :], self.st_lruconv[l, s], r=[], w=["lct"])
                self.DMA("sp", self.dct[:, q * 12:(q + 1) * 12, :], self.st_dnconv[l, s], r=[], w=["dct"])

    def store_states(self, kind):
        for l in range(NL):
            for s in ([0] if kind == "p" else [0, 1]):
                o = 0 if kind == "p" else 1 + s
                q = l * 2 + s
                self.DMA("sp", self.o_ret[l, o].rearrange("h k v -> k h v"), self.Sret[:, q, :].rearrange("p (h e) -> p h e", h=4), r=["Sret"], w=[])
                self.DMA("sp", self.o_dn[l, o].rearrange("h k v -> k h v"), self.Sdn[:, q, :].rearrange("p (h e) -> p h e", h=4), r=["Sdn"], w=[])
                self.DMA("sp", self.o_lruh[l, o], self.hst[:, q, :], r=["hst"], w=[])
                self.DMA("sp", self.o_lruconv[l, o], self.lct[:, q * 4:(q + 1) * 4, :], r=["lct"], w=[])
                self.DMA("sp", self.o_dnconv[l, o], self.dct[:, q * 12:(q + 1) * 12, :], r=["dct"], w=[])

    def ln_stats_chunk(self, m):
        T = self.T
        i = m % 2
        yk, qk = "o:ybf%d" % i, "o:ysq%d" % i
        self.ACT(self.ybf[i][:, 0:T], self.xres[:, m, 0:T], AF.Copy, r=["xres%d" % m], w=[yk])
        self.ACT(self.ysq[i][:, 0:T], self.xres[:, m, 0:T], AF.Square, r=["xres%d" % m], w=[qk])
        self.MM(self.pS1[:, 0:T], self.onesb[:], self.ybf[i][:, 0:T], m == 0, m == KC - 1, r=["onesb", yk], w=["pS1"])
        self.MM(self.pS2[:, 0:T], self.onesb[:], self.ysq[i][:, 0:T], m == 0, m == KC - 1, r=["onesb", qk], w=["pS2"])

    def ln_finish(self, l, which, eps):
        T = self.T
        mean, rstd, nmr = self.lnm[:, 0:T], self.lnr[:, 0:T], self.lnn[:, 0:T]
        t0 = self.tA[0][:, 0:T]
        self.ACT(mean, self.pS1[:, 0:T], AF.Copy, r=["pS1"], w=["lnm"], scale=1.0 / D)
        self.TT(t0, mean, mean, ALU.mult, r=["lnm"], w=["o:tA0"])
        self.STT(t0, self.pS2[:, 0:T], 1.0 / D, t0, ALU.mult, ALU.subtract, r=["pS2", "o:tA0"], w=["o:tA0"])
        self.rsqrt(rstd, t0, eps, r=["o:tA0"], w=["lnr"], tmp=self.tA[1][:, 0:T], tmpk="o:tA1")
        self.STT(nmr, mean, -1.0, rstd, ALU.mult, ALU.mult, r=["lnm", "lnr"], w=["lnn"])
        for m in range(KC):
            i = m % 2
            t = self.tA[i][:, 0:T]
            tk = "o:tA%d" % i
            self.TT(t, self.xres[:, m, 0:T], rstd, ALU.mult, r=["xres%d" % m, "lnr"], w=[tk])
            self.TT(t, t, nmr, ALU.add, r=[tk, "lnn"], w=[tk])
            gcol = self.ppfm[:, l, which * 16 + m: which * 16 + m + 1]
            bcol = self.ppfm[:, l, (which + 1) * 16 + m: (which + 1) * 16 + m + 1]
            self.ACT(self.xres[:, m, 0:T], t, AF.Identity, r=[tk, "ppfm"], w=["xres%d" % m], bias=bcol, scale=gcol)
            self.ACT(self.xbf[:, m, 0:T], self.xres[:, m, 0:T], AF.Copy, r=["xres%d" % m], w=["xbf%d" % m])

    def ffn(self, l, wi, wo, which):
        T = self.T
        self.barrier()
        hb = self.ovl[:, 0:HC * 512].rearrange("p (j t) -> p j t", j=HC)
        order = []
        for j in range(HC):
            order.append((wi, l, j * 128, 128))
            order.append((wi, l, DFF + j * 128, 128))
        self.pj_begin(order)
        for j in range(HC):
            sg = self.pj_next()
            su = self.pj_next()
            if j == HC - 2:
                self.load_wo(wo, l, 0, 0)
                self.load_wo(wo, l, 1, 1)
            pg, pgk = self.ps()
            pu, puk = self.ps()
            for kc in range(KC):
                self.MM(pg[:, 0:T], self.wring[sg][:, kc, :], self.xbf[:, kc, 0:T],
                        kc == 0, kc == KC - 1, r=["wr%d" % sg, "xbf%d" % kc], w=[pgk])
            for kc in range(KC):
                self.MM(pu[:, 0:T], self.wring[su][:, kc, :], self.xbf[:, kc, 0:T],
                        kc == 0, kc == KC - 1, r=["wr%d" % su, "xbf%d" % kc], w=[puk])
            si = j % 2
            sk = "o:sil%d" % si
            self.ACT(self.sil[si][:, 0:T], pg[:, 0:T], AF.Silu, r=[pgk], w=[sk])
            self.TT(hb[:, j, 0:T], self.sil[si][:, 0:T], pu[:, 0:T], ALU.mult, r=[sk, puk], w=["o:h%d" % j])
        for m in range(KC):
            s = m % 2
            po, pok = self.ps()
            for fc in range(HC):
                self.MM(po[:, 0:T], self.woring[s][:, fc, :], hb[:, fc, 0:T], fc == 0, fc == HC - 1,
                        r=["wo%d" % s, "o:h%d" % fc], w=[pok])
            if m + 2 < KC:
                self.load_wo(wo, l, m + 2, s)
            self.STT(self.xres[:, m, 0:T], self.xres[:, m, 0:T], 2.0 * ALPHA, po[:, 0:T], ALU.mult, ALU.add,
                     r=["xres%d" % m, pok], w=["xres%d" % m])
            self.ln_stats_chunk(m)
        self.ln_finish(l, which, 4.0 * EPS)

    def gelu(self, dst, src, tmp, r, w, tk):
        self.TT(tmp, src, src, ALU.mult, r=r, w=[tk])
        self.TS(tmp, tmp, 0.044715, 1.0, ALU.mult, ALU.add, r=[tk], w=[tk])
        self.TT(tmp, tmp, src, ALU.mult, r=[tk] + r, w=[tk])
        self.ACT(tmp, tmp, AF.Sigmoid, r=[tk], w=[tk], scale=1.5957691216057308)
        self.TT(dst, src, tmp, ALU.mult, r=[tk] + r, w=w)

    def mixer(self, l):
        T = self.T
        self.barrier()
        self.mg = self.ovl[:, 0:KC * 512].rearrange("p (m t) -> p m t", m=KC)
        self.DMA("sp", self.pprow[:], self.pp_row[l], r=[], w=["pprow"])
        order = []

        def cols(c0, n):
            for i in range(n):
                order.append(("wmi", l, c0 + i * 128, 128))
        if "A" in self.groups:
            cols(0, 16)
        if "B" in self.groups:
            for c in range(4):
                order.append(("wmi", l, 2560 + c * 128, 128))
                order.append(("wmi", l, 2048 + c * 128, 128))
        if "C" in self.groups:
            cols(3584, 4)
            cols(3072, 4)
        if "D" in self.groups:
            order.append(("wmi", l, 6144, 8))
            cols(4096, 16)
        for m in range(KC):
            order.append(("wmo", l, m * 128, 128))
        self.pj_begin(order)
        for gi, g in enumerate("ABCD"):
            if g in self.groups:
                getattr(self, "grp" + g)(l)
            else:
                for c in range(4):
                    self.MEMSET(self.mg[:, gi * 4 + c, 0:T], 0.0, w=["o:mg%d" % (gi * 4 + c)])
            self.barrier()
        if self.cfg.get("dbg") and self.kind == "p" and l == self.depth - 1:
            self.barrier()
            t, tk = (self.ovl32[:, 16384 // 4:(16384 + KC * 512 * 4) // 4].rearrange("p (m t) -> p m t", m=KC), "o:dbg")
            for m in range(KC):
                self.CP(t[:, m, :], self.mg[:, m, :], r=["o:mg%d" % m], w=[tk])
            self.DMA("sp", self.dbg_mg, t, r=[tk], w=[])
            self.barrier()
        for m in range(KC):
            slot = self.pj_next()
            po, pok = self.ps()
            for kc in range(KC):
                self.MM(po[:, 0:T], self.wring[slot][:, kc, :], self.mg[:, kc, 0:T], kc == 0, kc == KC - 1,
                        r=["wr%d" % slot, "o:mg%d" % kc], w=[pok])
            self.STT(self.xres[:, m, 0:T], self.xres[:, m, 0:T], ALPHA, po[:, 0:T], ALU.mult, ALU.add,
                     r=["xres%d" % m, pok], w=["xres%d" % m])
            self.ln_stats_chunk(m)
        self.ln_finish(l, 2, EPS)

    def grpA(self, l):
        T = self.T
        ki = self.ki
        self.mt_reset()
        qrot, qrk = self.mt([128, 4, T], BF16)
        qdec, qdk = self.mt([128, 4, T], BF16)
        krot, krk = self.mt([128, 4, T], BF16)
        vT, vTk = self.mt([128, 4, T], BF16)
        gate, gk = self.mt([128, 4, T], BF16)
        X, Xk = self.mt([128, T], F32)
        t1, t1k = self.mt([128, T], F32)
        t2, t2k = self.mt([128, T], F32)
        cq = 128 if self.kind == "p" else 32
        for which in range(2):
            for h in range(4):
                pb, pk = self.proj()
                self.CP(X, pb[:, 0:T], r=[pk], w=[Xk], eng="act")
                self.TT(t1, X, self.cosb[:, 0:T], ALU.mult, r=[Xk, "cosb"], w=[t1k])
                self.TT(t2[0:64], X[64:128], self.sinb[64:128, 0:T], ALU.mult, r=[Xk, "sinb"], w=[t2k + "a"])
                self.TT(t2[64:128], X[0:64], self.sinb[0:64, 0:T], ALU.mult, r=[Xk, "sinb"], w=[t2k + "b"])
                self.TT(t1, t1, t2, ALU.add, r=[t1k, t2k + "a", t2k + "b"], w=[t1k])
                if which == 0:
                    self.CP(qrot[:, h, :], t1, r=[t1k], w=[qrk], eng="act")
                    for (c0, L, slot) in self.segs:
                        for ci in range(L // cq):
                            a = c0 + ci * cq
                            self.TT(qdec[:, h, a:a + cq], t1[:, a:a + cq], self.rt[:, ki * 8 + 4 + h, 0:cq], ALU.mult,
                                    r=[t1k, "rt"], w=[qdk])
                else:
                    self.CP(krot[:, h, :], t1, r=[t1k], w=[krk], eng="act")
        for h in range(4):
            pb, pk = self.proj()
            self.CP(vT[:, h, :], pb[:, 0:T], r=[pk], w=[vTk], eng="act")
        for h in range(4):
            pb, pk = self.proj()
            self.ACT(gate[:, h, :], pb[:, 0:T], AF.Silu, r=[pk], w=[gk])
        Sbf, Sbk = self.mt([128, 4, 128], BF16)
        kTM, kTk = self.mt([128, 4, 128], BF16)
        vTM, vMk = self.mt([128, 512], BF16)
        scb, sck = self.mt([128, 4, 128], BF16)
        o32, o3k = self.mt([128, 4, 128], F32)
        sq, sqk = self.mt([128, 4, 128], F32)
        st, stk = self.mt([128, 16], F32)
        gam = self.hc["gam"][ki]
        for (c0, L, slot) in self.segs:
            S32 = self.Sret[:, l * 2 + slot, :].rearrange("p (h e) -> p h e", h=4)
            self.CP(Sbf, S32, r=["Sret"], w=[Sbk], eng="act")
            for ci in range(L // cq):
                a = c0 + ci * cq
                c = cq
                pb, pk = self.ps()
                for h in range(4):
                    self.MM(pb[0:c, h * 128:(h + 1) * 128], krot[:, h, a:a + c], self.identb[:], True, True,
                            r=[krk, "identb"], w=[pk])
                for h in range(4):
                    self.TS(kTM[0:c, h, :], pb[0:c, h * 128:(h + 1) * 128], self.kd[0:c, ki, h:h + 1], None, ALU.mult, None,
                            r=[pk, "kd"], w=[kTk])
                pb2, pk2 = self.ps()
                for h in range(4):
                    self.MM(pb2[0:c, h * 128:(h + 1) * 128], vT[:, h, a:a + c], self.identb[:], True, True,
                            r=[vTk, "identb"], w=[pk2])
                self.CP(vTM[0:c, :], pb2[0:c, :], r=[pk2], w=[vMk], eng="act")
                pb3, pk3 = self.ps()
                for h in range(4):
                    self.MM(pb3[0:c, h * 128:h * 128 + c], krot[:, h, a:a + c], qrot[:, h, a:a + c], True, True,
                            r=[krk, qrk], w=[pk3])
                p3v = pb3[:].rearrange("p (h i) -> p h i", h=4)
                self.TT(scb[0:c, :, 0:c], p3v[0:c, :, 0:c], self.rt[0:c, ki * 8:ki * 8 + 4, 0:c], ALU.mult, r=[pk3, "rt"], w=[sck])
                pb4, pk4 = self.ps()
                for h in range(4):
                    self.MM(pb4[0:c, h * 128:(h + 1) * 128], scb[0:c, h, 0:c], vTM[0:c, h * 128:(h + 1) * 128], True, False,
                            r=[sck, vMk], w=[pk4])
                    self.MM(pb4[0:c, h * 128:(h + 1) * 128], qdec[:, h, a:a + c], Sbf[:, h, :], False, True,
                            r=[qdk, Sbk], w=[pk4])
                p4v = pb4[:].rearrange("p (h e) -> p h e", h=4)
                self.CP(o32[0:c], p4v[0:c], r=[pk4], w=[o3k], eng="act")
                self.RSUM(st[0:c, 0:4], o32[0:c], r=[o3k], w=[stk])
                self.TT(sq[0:c], o32[0:c], o32[0:c], ALU.mult, r=[o3k], w=[sqk])
                self.RSUM(st[0:c, 4:8], sq[0:c], r=[sqk], w=[stk])
                self.TS(st[0:c, 0:4], st[0:c, 0:4], 1.0 / 128, None, ALU.mult, None, r=[stk], w=[stk])
                self.TT(st[0:c, 8:12], st[0:c, 0:4], st[0:c, 0:4], ALU.mult, r=[stk], w=[stk])
                self.STT(st[0:c, 4:8], st[0:c, 4:8], 1.0 / 128, st[0:c, 8:12], ALU.mult, ALU.subtract, r=[stk], w=[stk])
                self.rsqrt(st[0:c, 4:8], st[0:c, 4:8], EPS, r=[stk], w=[stk], tmp=st[0:c, 12:16], tmpk=stk)
                for h in range(4):
                    self.TS(o32[0:c, h, :], o32[0:c, h, :], st[0:c, h:h + 1], st[0:c, 4 + h:5 + h], ALU.subtract, ALU.mult,
                            r=[o3k, stk], w=[o3k])
                pb5, pk5 = self.ps()
                for h in range(4):
                    self.MM(pb5[:, h * 128:h * 128 + c], o32[0:c, h, :], self.ident32[0:c, 0:c], True, True,
                            r=[o3k, "ident32"], w=[pk5])
                for h in range(4):
                    self.STT(self.mg[:, h, a:a + c], pb5[:, h * 128:h * 128 + c], self.ppfm[:, l, 176 + h:177 + h],
                             gate[:, h, a:a + c], ALU.mult, ALU.mult, r=[pk5, "ppfm", gk], w=["o:mg%d" % h])
                pb6, pk6 = self.ps()
                for h in range(4):
                    self.MM(pb6[:, h * 128:(h + 1) * 128], kTM[0:c, h, :], vTM[0:c, h * 128:(h + 1) * 128], True, True,
                            r=[kTk, vMk], w=[pk6])
                for h in range(4):
                    self.STT(S32[:, h, :], S32[:, h, :], gam[h], pb6[:, h * 128:(h + 1) * 128], ALU.mult, ALU.add,
                             r=["Sret", pk6], w=["Sret"])
                self.CP(Sbf, S32, r=["Sret"], w=[Sbk], eng="act")

    def grpB(self, l):
        T = self.T
        self.mt_reset()
        nseg = len(self.segs)
        L = self.segs[0][1]
        xpre, xpk = self.mt([128, nseg, L + 3], F32)
        xc, xck = self.mt([128, T], F32)
        xcb, xbk = self.mt([128, T], BF16)
        rr, rk = self.mt([128, T], F32)
        ii, ik = self.mt([128, T], F32)
        aa, ak = self.mt([128, T], F32)
        a2, a2k = self.mt([128, T], F32)
        hh, hk = self.mt([128, T], F32)
        yy, yk = self.mt([128, T], F32)
        tg, tgk = self.mt([128, T], F32)
        for c in range(4):
            pb, pk = self.proj()
            for si, (c0, L, slot) in enumerate(self.segs):
                self.CP(xpre[:, si, 0:3], self.lct[:, (l * 2 + slot) * 4 + c, :], r=["lct"], w=[xpk])
                self.CP(xpre[:, si, 3:3 + L], pb[:, c0:c0 + L], r=[pk], w=[xpk], eng="act")
                w0 = 96 + c * 4
                self.TS(xc[:, c0:c0 + L], xpre[:, si, 0:L], self.ppfm[:, l, w0:w0 + 1], self.ppfm[:, l, 112 + c:113 + c],
                        ALU.mult, ALU.add, r=[xpk, "ppfm"], w=[xck])
                for j in range(1, 4):
                    self.STT(xc[:, c0:c0 + L], xpre[:, si, j:j + L], self.ppfm[:, l, w0 + j:w0 + j + 1], xc[:, c0:c0 + L],
                             ALU.mult, ALU.add, r=[xpk, "ppfm", xck], w=[xck])
                self.CP(self.lct[:, (l * 2 + slot) * 4 + c, :], xpre[:, si, L:L + 3], r=[xpk], w=["lct"])
            self.CP(xcb, xc, r=[xck], w=[xbk], eng="act")
            pr, prk = self.ps()
            self.MM(pr[:, 0:T], self.bdb[:, l * 8 + c, :], xcb, True, True, r=["bdb", xbk], w=[prk])
            pi, pik = self.ps()
            self.MM(pi[:, 0:T], self.bdb[:, l * 8 + 4 + c, :], xcb, True, True, r=["bdb", xbk], w=[pik])
            self.ACT(rr, pr[:, 0:T], AF.Sigmoid, r=[prk, "ppfm"], w=[rk], bias=self.ppfm[:, l, 116 + c:117 + c])
            self.ACT(ii, pi[:, 0:T], AF.Sigmoid, r=[pik, "ppfm"], w=[ik], bias=self.ppfm[:, l, 120 + c:121 + c])
            self.ACT(aa, rr, AF.Exp, r=[rk, "lrup"], w=[ak], scale=self.lrup[:, l, c, 0:1])
            self.ACT(a2, rr, AF.Exp, r=[rk, "lrup"], w=[a2k], scale=self.lrup[:, l, c, 1:2])
            self.TS(a2, a2, -1.0, 1.0, ALU.mult, ALU.add, r=[a2k], w=[a2k])
            self.ACT(a2, a2, AF.Sqrt, r=[a2k], w=[a2k])
            self.TT(a2, a2, ii, ALU.mult, r=[a2k, ik], w=[a2k])
            self.TT(a2, a2, xc, ALU.mult, r=[a2k, xck], w=[a2k])
            for si, (c0, L, slot) in enumerate(self.segs):
                self.SCAN(hh[:, c0:c0 + L], aa[:, c0:c0 + L], a2[:, c0:c0 + L], self.hst[:, l * 2 + slot, c:c + 1],
                          r=[ak, a2k, "hst"], w=[hk])
                self.CP(self.hst[:, l * 2 + slot, c:c + 1], hh[:, c0 + L - 1:c0 + L], r=[hk], w=["hst"])
            pb2, pk2 = self.proj()
            self.CP(yy, pb2[:, 0:T], r=[pk2], w=[yk], eng="act")
            self.gelu(yy, yy, tg, r=[yk], w=[yk], tk=tgk)
            self.TT(self.mg[:, 4 + c, 0:T], hh, yy, ALU.mult, r=[hk, yk], w=["o:mg%d" % (4 + c)])

    def grpC(self, l):
        T = self.T
        self.mt_reset()
        gv, gvk = self.mt([128, 4, T], F32)
        sT, sTk = self.mt([128, 4, T], F32)
        yy, yk = self.mt([128, T], F32)
        tg, tgk = self.mt([128, T], F32)
        vpre, vpk = self.mt([128, 512], F32)
        vsq, vqk = self.mt([128, 512], F32)
        vnb, vbk = self.mt([128, 512], BF16)
        st, stk = self.mt([128, 8], F32)
        for c in range(4):
            pb, pk = self.proj()
            self.CP(yy, pb[:, 0:T], r=[pk], w=[yk], eng="act")
            self.gelu(gv[:, c, :], yy, tg, r=[yk], w=[gvk], tk=tgk)
        cg = 128 if self.kind == "p" else 32
        for si, (c0, L, slot) in enumerate(self.segs):
            for ci in range(L // cg):
                a = c0 + ci * cg
                c = cg
                pb, pk = self.ps()
                for q in range(4):
                    self.MM(pb[0:c, q * 128:(q + 1) * 128], gv[:, q, a:a + c], self.ident32[:], True, True,
                            r=[gvk, "ident32"], w=[pk])
                self.CP(vpre[0:c], pb[0:c, :], r=[pk], w=[vpk], eng="act")
                self.RSUM(st[0:c, 0:1], vpre[0:c], r=[vpk], w=[stk])
                self.TT(vsq[0:c], vpre[0:c], vpre[0:c], ALU.mult, r=[vpk], w=[vqk])
                self.RSUM(st[0:c, 1:2], vsq[0:c], r=[vqk], w=[stk])
                self.TS(st[0:c, 0:1], st[0:c, 0:1], 1.0 / 512, None, ALU.mult, None, r=[stk], w=[stk])
                self.TT(st[0:c, 2:3], st[0:c, 0:1], st[0:c, 0:1], ALU.mult, r=[stk], w=[stk])
                self.STT(st[0:c, 1:2], st[0:c, 1:2], 1.0 / 512, st[0:c, 2:3], ALU.mult, ALU.subtract, r=[stk], w=[stk])
                self.rsqrt(st[0:c, 1:2], st[0:c, 1:2], EPS, r=[stk], w=[stk], tmp=st[0:c, 3:4], tmpk=stk)
                self.TS(vpre[0:c], vpre[0:c], st[0:c, 0:1], st[0:c, 1:2], ALU.subtract, ALU.mult, r=[vpk, stk], w=[vpk])
                self.TT(vpre[0:c], vpre[0:c], self.pprow[0:c, 0:512], ALU.mult, r=[vpk, "pprow"], w=[vpk])
                self.TT(vpre[0:c], vpre[0:c], self.pprow[0:c, 512:1024], ALU.add, r=[vpk, "pprow"], w=[vpk])
                if self.kind == "s":
                    self.DMA("sp", self.o_sgv[l, si], vpre[0:c], r=[vpk], w=[])
                self.CP(vnb[0:c], vpre[0:c], r=[vpk], w=[vbk], eng="act")
                pb2, pk2 = self.ps()
                for h in range(4):
                    wT = self.sgwb[:, l, h, :] if self.kind == "p" else self.sgwsb[:, l, h, :]
                    self.MM(pb2[:, h * 128:h * 128 + c], vnb[0:c, h * 128:(h + 1) * 128], wT, True, True,
                            r=[vbk, "sgwb", "sgwsb"], w=[pk2])
                p2v = pb2[:].rearrange("p (h i) -> p h i", h=4)
                brow = self.pprow[:, 1024:1536].rearrange("p (h i) -> p h i", h=4)
                self.TT(sT[:, :, a:a + c], p2v[:, :, 0:c], brow[:, :, 0:c], ALU.add, r=[pk2, "pprow"], w=[sTk])
        for c in range(4):
            pb, pk = self.proj()
            self.CP(yy, pb[:, 0:T], r=[pk], w=[yk], eng="act")
            self.gelu(yy, yy, tg, r=[yk], w=[yk], tk=tgk)
            self.TT(self.mg[:, 8 + c, 0:T], yy, sT[:, c, :], ALU.mult, r=[yk, sTk], w=["o:mg%d" % (8 + c)])

    def grpD(self, l):
        T = self.T
        self.mt_reset()
        nseg = len(self.segs)
        Lseg = self.segs[0][1]
        cd = 64 if self.kind == "p" else 32
        nlev = 5 if cd == 64 else 4
        QT, QTk = self.mt([128, 4, T], BF16)
        KT, KTk = self.mt([128, 4, T], BF16)
        VT, VTk = self.mt([128, 4, T], BF16)
        grow, grk = self.mt([4, T], F32)
        gcrow, gck = self.mt([4, T], F32)
        brow, brk = self.mt([4, T], F32)
        mark = self._mt
        slot = self.pj_next()
        pa, pak = self.ps()
        for kc in range(KC):
            self.MM(pa[0:4, 0:T], self.wring[slot][:, kc, 0:4], self.xbf[:, kc, 0:T], kc == 0, kc == KC - 1,
                    r=["wr%d" % slot, "xbf%d" % kc], w=[pak])
        pbb, pbk = self.ps()
        for kc in range(KC):
            self.MM(pbb[0:4, 0:T], self.wring[slot][:, kc, 4:8], self.xbf[:, kc, 0:T], kc == 0, kc == KC - 1,
                    r=["wr%d" % slot, "xbf%d" % kc], w=[pbk])
        self.ACT(grow, pa[0:4, 0:T], AF.Exp, r=[pak, "dn4"], w=[grk], bias=self.dn4[:, l, 1:2])
        self.ACT(grow, grow, AF.Ln, r=[grk], w=[grk], bias=1.0)
        self.TS(grow, grow, self.dn4[:, l, 2:3], None, ALU.mult, None, r=[grk, "dn4"], w=[grk])
        self.ACT(brow, pbb[0:4, 0:T], AF.Sigmoid, r=[pbk], w=[brk])
        for a in range(0, T, cd):
            self.SCAN(gcrow[:, a:a + cd], self.ones4[0:4, 0:cd], grow[:, a:a + cd], 0.0, r=[grk, "ones4"], w=[gck])
        xpre, xpk = self.mt([128, nseg, Lseg + 3], F32)
        y32, y3k = self.mt([128, T], F32)
        sqb, sqk = self.mt([128, T], BF16)
        rn, rnk = self.mt([128, T], F32)
        rt_, rtk = self.mt([128, T], F32)
        for c in range(12):
            pb, pk = self.proj()
            for si, (c0, L, slot_) in enumerate(self.segs):
                self.CP(xpre[:, si, 0:3], self.dct[:, (l * 2 + slot_) * 12 + c, :], r=["dct"], w=[xpk])
                self.CP(xpre[:, si, 3:3 + L], pb[:, c0:c0 + L], r=[pk], w=[xpk], eng="act")
                w0 = 128 + c * 4
                self.TS(y32[:, c0:c0 + L], xpre[:, si, 0:L], self.ppfm[:, l, w0:w0 + 1], None, ALU.mult, None,
                        r=[xpk, "ppfm"], w=[y3k])
                for j in range(1, 4):
                    self.STT(y32[:, c0:c0 + L], xpre[:, si, j:j + L], self.ppfm[:, l, w0 + j:w0 + j + 1], y32[:, c0:c0 + L],
                             ALU.mult, ALU.add, r=[xpk, "ppfm", y3k], w=[y3k])
                self.CP(self.dct[:, (l * 2 + slot_) * 12 + c, :], xpre[:, si, L:L + 3], r=[xpk], w=["dct"])
            self.ACT(y32, y32, AF.Silu, r=[y3k], w=[y3k])
            h = c % 4
            if c < 8:
                self.ACT(sqb, y32, AF.Square, r=[y3k], w=[sqk])
                pq, pqk = self.ps()
                self.MM(pq[:, 0:T], self.onesb[:], sqb, True, True, r=["onesb", sqk], w=[pqk])
                self.rsqrt(rn, pq[:, 0:T], 1e-6, r=[pqk], w=[rnk], tmp=rt_, tmpk=rtk)
                if c < 4:
                    self.STT(QT[:, h, :], y32, 128 ** -0.5, rn, ALU.mult, ALU.mult, r=[y3k, rnk], w=[QTk])
                else:
                    self.TT(KT[:, h, :], y32, rn, ALU.mult, r=[y3k, rnk], w=[KTk])
            else:
                self.CP(VT[:, h, :], y32, r=[y3k], w=[VTk], eng="act")
        def dump(i):
            if self.cfg.get("dbg") and self.kind == "p" and l == 0 and self.ti == 0:
                for j, (t_, k_) in enumerate(((QT, QTk), (KT, KTk), (VT, VTk))):
                    self.DMA("sp", self.dbg_q[i][:, j * 4:(j + 1) * 4, :], t_, r=[k_], w=[])
        dump(0)
        self.barrier()
        self._mt = mark
        H = 4
        colq, cqk = self.mt([64, 16], F32)
        kdc, kdk = self.mt([64, 4], F32)
        egl, eglk = self.mt([128, 4], F32)
        gcB, gBk = self.mt([128, 4, 64], F32)
        egB, eBk = self.mt([128, 4, 64], F32)
        msk, mkk = self.mt([4, 4, 64], F32)
        Dm, Dmk = self.mt([64, 4, 64], F32)
        DTm, DTk = self.mt([64, 4, 64], F32)
        Nm = [self.mt([64, 4, 64], F32) for _ in range(2)]
        NTm = [self.mt([64, 4, 64], F32) for _ in range(2)]
        XT, XTk = self.mt([64, 4, 64], F32)
        XTb, XTbk = self.mt([64, 4, 64], BF16)
        atb, atk = self.mt([64, 4, 64], BF16)
        bV, bVk = self.mt([64, 4, 128], BF16)
        Kbg, Kbk = self.mt([64, 4, 128], BF16)
        Kd, Kdk = self.mt([64, 4, 128], BF16)
        nwT, nwk = self.mt([128, 4, 64], BF16)
        QgT, Qgk = self.mt([128, 4, 64], BF16)
        vnew, vnk = self.mt([64, 4, 128], BF16)
        o32, o3k = self.mt([64, 4, 128], F32)
        osq, oqk = self.mt([64, 4, 128], F32)
        st, stk = self.mt([64, 12], F32)
        Sbf, Sbk = self.mt([128, 4, 128], BF16)
        Lm = self.mask[0:cd, 1, 0:cd]
        Um = self.mask[0:cd, 2, 0:cd]
        NCH = self.cfg.get("dn_chains", 2)
        HPC = 4 // NCH

        def chain(ch, a, c, S32, first):
            hs = list(range(ch * HPC, (ch + 1) * HPC))
            h0, h1 = hs[0], hs[-1] + 1
            nh = len(hs)
            sfx = "_c%d" % ch

            def K(k):
                return k + sfx
            SK = "Sdn" + sfx
            pc, pck = self.ps()
            self.MM(pc[0:c, 0:nh], gcrow[0:4, a:a + c], self.ident32[0:4, h0:h1], True, True, r=[gck, "ident32"], w=[pck])
            self.MM(pc[0:c, 4:4 + nh], brow[0:4, a:a + c], self.ident32[0:4, h0:h1], True, True, r=[brk, "ident32"], w=[pck])
            pbc, pbck = self.ps()
            for h in hs:
                self.TS(msk[:, h, 0:c], gcrow[0:4, a:a + c], self.ident32[0:4, h:h + 1], None, ALU.mult, None,
                        r=[gck, "ident32"], w=[K(mkk)])
                self.MM(pbc[:, (h - h0) * 64:(h - h0) * 64 + c], self.ones4[:], msk[:, h, 0:c], True, True,
                        r=["ones4", K(mkk)], w=[pbck])
            yield
            self.CP(colq[0:c, h0:h1], pc[0:c, 0:nh], r=[pck], w=[K(cqk)])
            self.CP(colq[0:c, 4 + h0:4 + h1], pc[0:c, 4:4 + nh], r=[pck], w=[K(cqk)])
            self.TS(colq[0:c, 8 + h0:8 + h1], colq[0:c, 4 + h0:4 + h1], -1.0, None, ALU.mult, None, r=[K(cqk)], w=[K(cqk)])
            self.ACT(colq[0:c, 12 + h0:12 + h1], colq[0:c, h0:h1], AF.Exp, r=[K(cqk)], w=[K(cqk)])
            pbv = pbc[:, 0:nh * 64].rearrange("p (h i) -> p h i", h=nh)
            self.CP(gcB[:, h0:h1, 0:c], pbv[:, :, 0:c], r=[pbck], w=[K(gBk)])
            yield
            self.ACT(egB[:, h0:h1, 0:c], gcB[:, h0:h1, 0:c], AF.Exp, r=[K(gBk)], w=[K(eBk)])
            self.TT(kdc[0:c, h0:h1], gcB[0:c, h0:h1, c - 1], colq[0:c, h0:h1], ALU.subtract, r=[K(gBk), K(cqk)], w=[K(kdk)])
            for h in hs:
                self.TS(Dm[0:c, h, 0:c], gcB[0:c, h, 0:c], colq[0:c, h:h + 1], -1.0, ALU.subtract, ALU.mult,
                        r=[K(gBk), K(cqk)], w=[K(Dmk)])
                self.TS(DTm[0:c, h, 0:c], gcB[0:c, h, 0:c], colq[0:c, h:h + 1], 0.0, ALU.subtract, ALU.min,
                        r=[K(gBk), K(cqk)], w=[K(DTk)])
            self.TS(Dm[0:c, h0:h1, 0:c], Dm[0:c, h0:h1, 0:c], 0.0, None, ALU.min, None, r=[K(Dmk)], w=[K(Dmk)])
            pG, pGk = self.ps()
            for h in hs:
                hi = h - h0
                self.MM(pG[0:c, hi * 64:hi * 64 + c], KT[:, h, a:a + c], KT[:, h, a:a + c], True, True, r=[KTk], w=[pGk])
                self.MM(pG[0:c, 256 + hi * 64:256 + hi * 64 + c], KT[:, h, a:a + c], QT[:, h, a:a + c], True, True,
                        r=[KTk, QTk], w=[pGk])
            yield
            self.CP(egl[:, h0:h1], egB[:, h0:h1, c - 1], r=[K(eBk)], w=[K(eglk)])
            self.ACT(kdc[0:c, h0:h1], kdc[0:c, h0:h1], AF.Exp, r=[K(kdk)], w=[K(kdk)])
            self.ACT(Dm[0:c, h0:h1, 0:c], Dm[0:c, h0:h1, 0:c], AF.Exp, r=[K(Dmk)], w=[K(Dmk)])
            self.ACT(DTm[0:c, h0:h1, 0:c], DTm[0:c, h0:h1, 0:c], AF.Exp, r=[K(DTk)], w=[K(DTk)])
            yield
            for h in hs:
                self.TT(Dm[0:c, h, 0:c], Dm[0:c, h, 0:c], Lm, ALU.mult, r=[K(Dmk), "mask"], w=[K(Dmk)])
                self.TT(DTm[0:c, h, 0:c], DTm[0:c, h, 0:c], Um, ALU.mult, r=[K(DTk), "mask"], w=[K(DTk)])
            (N0, N0k), (N1, N1k) = Nm
            (NT0, NT0k), (NT1, NT1k) = NTm
            for h in hs:
                hi = h - h0
                self.STT(N0[0:c, h, 0:c], pG[0:c, hi * 64:hi * 64 + c], colq[0:c, 8 + h:9 + h], Dm[0:c, h, 0:c],
                         ALU.mult, ALU.mult, r=[pGk, K(cqk), K(Dmk)], w=[K(N0k)])
            pGv = pG[:, 256:256 + nh * 64].rearrange("p (h i) -> p h i", h=nh)
            self.TT(atb[0:c, h0:h1, 0:c], pGv[0:c, :, 0:c], DTm[0:c, h0:h1, 0:c], ALU.mult, r=[pGk, K(DTk)], w=[K(atk)])
            pN, pNk = self.ps()
            for h in hs:
                hi = h - h0
                self.MM(pN[0:c, hi * 64:hi * 64 + c], N0[0:c, h, 0:c], self.ident32[0:c, 0:c], True, True,
                        r=[K(N0k), "ident32"], w=[pNk])
            yield
            pNv = pN[:, 0:nh * 64].rearrange("p (h i) -> p h i", h=nh)
            self.CP(NT0[0:c, h0:h1, 0:c], pNv[0:c, :, 0:c], r=[pNk], w=[K(NT0k)], eng="act")
            yield
            for h in hs:
                self.TT(XT[0:c, h, 0:c], NT0[0:c, h, 0:c], self.ident32[0:c, 0:c], ALU.add, r=[K(NT0k), "ident32"], w=[K(XTk)])
            Pc, Pck, PTc, PTck = N0, N0k, NT0, NT0k
            Pn, Pnk, PTn, PTnk = N1, N1k, NT1, NT1k
            for lev in range(nlev):
                pP, pPk = self.ps()
                for h in hs:
                    hi = h - h0
                    self.MM(pP[0:c, hi * 64:hi * 64 + c], PTc[0:c, h, 0:c], Pc[0:c, h, 0:c], True, True, r=[K(PTck), K(Pck)], w=[pPk])
                    if lev < nlev - 1:
                        self.MM(pP[0:c, 256 + hi * 64:256 + hi * 64 + c], Pc[0:c, h, 0:c], PTc[0:c, h, 0:c], True, True,
                                r=[K(PTck), K(Pck)], w=[pPk])
                yield
                pPa = pP[:, 0:nh * 64].rearrange("p (h i) -> p h i", h=nh)
                pPb = pP[:, 256:256 + nh * 64].rearrange("p (h i) -> p h i", h=nh)
                self.CP(Pn[0:c, h0:h1, 0:c], pPa[0:c, :, 0:c], r=[pPk], w=[K(Pnk)])
                if lev < nlev - 1:
                    self.CP(PTn[0:c, h0:h1, 0:c], pPb[0:c, :, 0:c], r=[pPk], w=[K(PTnk)], eng="act")
                yield
                pX, pXk = self.ps()
                for h in hs:
                    hi = h - h0
                    self.MM(pX[0:c, hi * 64:hi * 64 + c], Pn[0:c, h, 0:c], XT[0:c, h, 0:c], True, True, r=[K(Pnk), K(XTk)], w=[pXk])
                yield
                pXv = pX[:, 0:nh * 64].rearrange("p (h i) -> p h i", h=nh)
                self.TT(XT[0:c, h0:h1, 0:c], XT[0:c, h0:h1, 0:c], pXv[0:c, :, 0:c], ALU.add, r=[K(XTk), pXk], w=[K(XTk)])
                Pc, Pck, PTc, PTck, Pn, Pnk, PTn, PTnk = Pn, Pnk, PTn, PTnk, Pc, Pck, PTc, PTck
            pK, pKk = self.ps()
            for h in hs:
                hi = h - h0
                self.MM(pK[0:c, hi * 128:(hi + 1) * 128], KT[:, h, a:a + c], self.identb[:], True, True, r=[KTk, "identb"], w=[pKk])
                self.MM(pK[0:c, 256 + hi * 128:256 + (hi + 1) * 128], VT[:, h, a:a + c], self.identb[:], True, True,
                        r=[VTk, "identb"], w=[pKk])
            yield
            self.CP(XTb[0:c, h0:h1, 0:c], XT[0:c, h0:h1, 0:c], r=[K(XTk)], w=[K(XTbk)], eng="act")
            for h in hs:
                hi = h - h0
                self.TS(Kbg[0:c, h, :], pK[0:c, hi * 128:(hi + 1) * 128], colq[0:c, 4 + h:5 + h], colq[0:c, 12 + h:13 + h],
                        ALU.mult, ALU.mult, r=[pKk, K(cqk)], w=[K(Kbk)])
                self.TS(Kd[0:c, h, :], pK[0:c, hi * 128:(hi + 1) * 128], kdc[0:c, h:h + 1], None, ALU.mult, None,
                        r=[pKk, K(kdk)], w=[K(Kdk)])
                self.TS(bV[0:c, h, :], pK[0:c, 256 + hi * 128:256 + (hi + 1) * 128], colq[0:c, 4 + h:5 + h], None, ALU.mult, None,
                        r=[pKk, K(cqk)], w=[K(bVk)])
            self.TT(QgT[:, h0:h1, 0:c], QT[:, h0:h1, a:a + c], egB[:, h0:h1, 0:c], ALU.mult, r=[QTk, K(eBk)], w=[K(Qgk)])
            yield
            pW, pWk = self.ps()
            for h in hs:
                hi = h - h0
                self.MM(pW[:, hi * 64:hi * 64 + c], Kbg[0:c, h, :], XTb[0:c, h, 0:c], True, True, r=[K(Kbk), K(XTbk)], w=[pWk])
            yield
            pWv = pW[:, 0:nh * 64].rearrange("p (h i) -> p h i", h=nh)
            self.ACT(nwT[:, h0:h1, 0:c], pWv[:, :, 0:c], AF.Copy, r=[pWk], w=[K(nwk)], scale=-1.0)
            yield
            if first:
                self.CP(Sbf[:, h0:h1, :], S32[:, h0:h1, :], r=["Sdn", SK], w=[K(Sbk)], eng="act")
                yield
            pU, pUk = self.ps()
            for h in hs:
                hi = h - h0
                self.MM(pU[0:c, hi * 128:(hi + 1) * 128], XTb[0:c, h, 0:c], bV[0:c, h, :], True, False, r=[K(XTbk), K(bVk)], w=[pUk])
                self.MM(pU[0:c, hi * 128:(hi + 1) * 128], nwT[:, h, 0:c], Sbf[:, h, :], False, True, r=[K(nwk), K(Sbk)], w=[pUk])
            yield
            pUv = pU[:, 0:nh * 128].rearrange("p (h e) -> p h e", h=nh)
            self.CP(vnew[0:c, h0:h1, :], pUv[0:c], r=[pUk], w=[K(vnk)], eng="act")
            yield
            pO, pOk = self.ps()
            for h in hs:
                hi = h - h0
                self.MM(pO[0:c, hi * 128:(hi + 1) * 128], QgT[:, h, 0:c], Sbf[:, h, :], True, False, r=[K(Qgk), K(Sbk)], w=[pOk])
                self.MM(pO[0:c, hi * 128:(hi + 1) * 128], atb[0:c, h, 0:c], vnew[0:c, h, :], False, True, r=[K(atk), K(vnk)], w=[pOk])
                self.MM(pO[:, 256 + hi * 128:256 + (hi + 1) * 128], Kd[0:c, h, :], vnew[0:c, h, :], True, True, r=[K(Kdk), K(vnk)], w=[pOk])
            yield
            pOv = pO[:, 0:nh * 128].rearrange("p (h e) -> p h e", h=nh)
            self.CP(o32[0:c, h0:h1, :], pOv[0:c], r=[pOk], w=[K(o3k)], eng="act")
            for h in hs:
                hi = h - h0
                self.STT(S32[:, h, :], S32[:, h, :], egl[:, h:h + 1], pO[:, 256 + hi * 128:256 + (hi + 1) * 128], ALU.mult, ALU.add,
                         r=["Sdn", SK, K(eglk), pOk], w=[SK])
            yield
            self.CP(Sbf[:, h0:h1, :], S32[:, h0:h1, :], r=[SK], w=[K(Sbk)], eng="act")
            self.TT(osq[0:c, h0:h1, :], o32[0:c, h0:h1, :], o32[0:c, h0:h1, :], ALU.mult, r=[K(o3k)], w=[K(oqk)])
            self.RSUM(st[0:c, h0:h1], osq[0:c, h0:h1, :], r=[K(oqk)], w=[K(stk)])
            self.rsqrt(st[0:c, 4 + h0:4 + h1], st[0:c, h0:h1], EPS, r=[K(stk)], w=[K(stk)], tmp=st[0:c, 8 + h0:8 + h1], tmpk=K(stk),
                       scale=1.0 / 128)
            for h in hs:
                self.TS(o32[0:c, h, :], o32[0:c, h, :], st[0:c, 4 + h:5 + h], None, ALU.mult, None, r=[K(o3k), K(stk)], w=[K(o3k)])
            yield
            pT, pTk = self.ps()
            for h in hs:
                hi = h - h0
                self.MM(pT[:, hi * 64:hi * 64 + c], o32[0:c, h, :], self.ident32[0:c, 0:c], True, True, r=[K(o3k), "ident32"], w=[pTk])
            yield
            for h in hs:
                hi = h - h0
                self.TS(self.mg[:, 12 + h, a:a + c], pT[:, hi * 64:hi * 64 + c], self.ppfm[:, l, 180 + h:181 + h], None,
                        ALU.mult, None, r=[pTk, "ppfm"], w=["o:mg%d" % (12 + h)])

        import itertools
        for (c0, L, slot_) in self.segs:
            S32 = self.Sdn[:, l * 2 + slot_, :].rearrange("p (h e) -> p h e", h=4)
            for ci in range(L // cd):
                a = c0 + ci * cd
                gens = [chain(ch, a, cd, S32, ci == 0) for ch in range(NCH)]
                for _ in itertools.zip_longest(*gens):
                    pass
            self.CP(self.bar[:, 0:1], self.bar[:, 0:1], r=["Sdn_c%d" % ch for ch in range(NCH)] + ["bar"], w=["Sdn", "bar"])
        dump(1)
        zz, zk = self.mt([128, T], F32)
        for h in range(4):
            pb, pk = self.proj()
            self.ACT(zz, pb[:, 0:T], AF.Silu, r=[pk], w=[zk])
            self.TT(self.mg[:, 12 + h, 0:T], self.mg[:, 12 + h, 0:T], zz, ALU.mult, r=[zk, "o:mg%d" % (12 + h)],
                    w=["o:mg%d" % (12 + h)])


def host_pack(inp):
    fm = np.zeros((NL, 128, NFM), np.float32)
    row = np.zeros((NL, 128, NROW), np.float32)
    bd = np.zeros((NL, 128, 2, 4, 128), np.float32)
    sgw = np.zeros((NL, 128, 4, 128), np.float32)
    sgws = np.zeros((NL, 32, 4, 32), np.float32)
    dn4 = np.zeros((NL, 4, 2), np.float32)

    def colmaj(v, n):
        return v.reshape(n, 128).T

    for l in range(NL):
        o = 0
        for nm in ["ln1_g", "ln1_b", "ln2_g", "ln2_b", "ln3_g", "ln3_b"]:
            fm[l, :, o:o + 16] = colmaj(inp[nm][l], 16)
            o += 16
        cw = inp["lru_conv_w"][l]
        for c in range(4):
            for j in range(4):
                fm[l, :, 96 + c * 4 + j] = cw[j, c * 128:(c + 1) * 128]
        fm[l, :, 112:116] = colmaj(inp["lru_conv_b"][l], 4)
        fm[l, :, 116:120] = colmaj(inp["lru_b_a"][l], 4)
        fm[l, :, 120:124] = colmaj(inp["lru_b_x"][l], 4)
        fm[l, :, 124:128] = colmaj(inp["lru_lam"][l], 4)
        dw = inp["dn_conv_w"][l]
        for c in range(12):
            for j in range(4):
                fm[l, :, 128 + c * 4 + j] = dw[j, c * 128:(c + 1) * 128]
        fm[l, :, 176:180] = colmaj(inp["ret_norm_g"][l], 4)
        for h in range(4):
            fm[l, :, 180 + h] = inp["dn_norm_g"][l]
        row[l, :, 0:512] = inp["sg_ln_g"][l][None, :]
        row[l, :, 512:1024] = inp["sg_ln_b"][l][None, :]
        row[l, :, 1024:1536] = inp["sg_b"][l].reshape(1, 512)
        for t, nm in enumerate(["lru_w_a", "lru_w_x"]):
            w = inp[nm][l]
            for c in range(4):
                bd[l, 0:64, t, c, 0:64] = w[2 * c]
                bd[l, 64:128, t, c, 64:128] = w[2 * c + 1]
        sgw[l] = inp["sg_w"][l].transpose(2, 0, 1)
        sgws[l] = inp["sg_w"][l][:, :32, :32].transpose(2, 0, 1)
        dn4[l, :, 0] = inp["dn_a_log"][l]
        dn4[l, :, 1] = inp["dn_dt_bias"][l]
    return dict(pp_fm=fm, pp_row=row, pp_bd=np.ascontiguousarray(bd.reshape(NL, 128, 8, 128)), pp_sgw=sgw, pp_sgws=sgws, pp_dn4=dn4)


def make_in_maps(inp, cfg):
    nt = cfg.get("ntiles", 4)
    LP = max(nt, 1) * 512
    hc = host_consts()
    pk = host_pack(inp)
    maps = []
    for c in range(8):
        m = {}
        m["xp"] = np.ascontiguousarray(inp["x_prompt"][c % 4, :LP])
        m["xs"] = np.ascontiguousarray(inp["x_sample"][2 * c:2 * c + 2].reshape(64, D))
        m["w1i"] = inp["ffn1_w_in"]
        m["w1o"] = inp["ffn1_w_out"]
        m["wmi"] = inp["w_mix_in"]
        m["wmo"] = inp["w_mix_out"]
        m["w2i"] = inp["ffn2_w_in"]
        m["w2o"] = inp["ffn2_w_out"]
        m.update(pk)
        for k in ["c_ident", "c_cos", "c_sin", "c_rt", "c_kd", "c_mask"]:
            m[k] = hc[k]
        sl = slice(2 * c, 2 * c + 2)
        m["st_ret"] = np.ascontiguousarray(inp["state_ret"][:, sl])
        m["st_dn"] = np.ascontiguousarray(inp["state_dn"][:, sl])
        m["st_lruh"] = np.ascontiguousarray(inp["state_lru_h"][:, sl].reshape(NL, 2, 4, 128).transpose(0, 1, 3, 2))
        m["st_lruconv"] = np.ascontiguousarray(inp["state_lru_conv"][:, sl].reshape(NL, 2, 3, 4, 128).transpose(0, 1, 4, 3, 2))
        m["st_dnconv"] = np.ascontiguousarray(inp["state_dn_conv"][:, sl].reshape(NL, 2, 3, 12, 128).transpose(0, 1, 4, 3, 2))
        maps.append(m)
    return maps


def assemble(results):
    yp = np.stack([results[c]["yp"] for c in range(4)])
    ys = np.concatenate([results[c]["ys"].reshape(2, 32, D) for c in range(8)], 0)

    def st(name, o):
        return np.stack([results[c][name][:, o] for c in range(4)], 1)

    def ss(name):
        return np.concatenate([results[c][name][:, 1:3] for c in range(8)], 1)

    def lruh(a):
        return np.ascontiguousarray(a.transpose(0, 1, 3, 2).reshape(a.shape[0], a.shape[1], 512))

    def conv(a, n):
        return np.ascontiguousarray(a.transpose(0, 1, 4, 3, 2).reshape(a.shape[0], a.shape[1], 3, n * 128))

    ret_p, dn_p = st("o_ret", 0), st("o_dn", 0)
    lruh_p = lruh(st("o_lruh", 0))
    lruc_p = conv(st("o_lruconv", 0), 4)
    dnc_p = conv(st("o_dnconv", 0), 12)
    ret_s, dn_s = ss("o_ret"), ss("o_dn")
    lruh_s = lruh(ss("o_lruh"))
    lruc_s = conv(ss("o_lruconv"), 4)
    dnc_s = conv(ss("o_dnconv"), 12)
    sgv = np.concatenate([results[c]["o_sgv"] for c in range(8)], 1)
    outs = (yp, ys, ret_p, lruh_p, lruc_p, dn_p, dnc_p, ret_s, lruh_s, lruc_s, dn_s, dnc_s, sgv)
    return tuple(np.ascontiguousarray(o, dtype=np.float32) for o in outs)


_CACHE = {}


def kernel(**inputs):
    inp = {k: np.asarray(v) for k, v in inputs.items()}
    cfg = dict(ntiles=4, depth=NL)
    if "nc" not in _CACHE:
        _CACHE["nc"] = Builder(cfg).build()
    nc = _CACHE["nc"]
    maps = make_in_maps(inp, cfg)
    res = run_bass_kernel_spmd(nc, maps, core_ids=list(range(8)))
    return assemble(res.results)
```

```python
import contextlib
import math
import numpy as np
import concourse.bass as bass
import concourse.mybir as mybir
from concourse.bass_utils import run_bass_kernel_spmd

F32 = mybir.dt.float32
BF16 = mybir.dt.bfloat16
ALU = mybir.AluOpType
AF = mybir.ActivationFunctionType
AX = mybir.AxisListType

D = 2048
DFF = 5632
G = 512
DIN = 6152
NL = 2
KC = 16
HC = 44
ALPHA = (2.0 * NL) ** 0.25
EPS = 1e-5
PAST = 4096
N_DMA_SEMS = 12


class Prog:
    ENGS = ["pe", "act", "dve", "pool", "sp"]

    def __init__(self, nc):
        self.nc = nc
        self.ops = []
        self.last_w = {}
        self.readers = {}

    def add(self, eng, fn, r=(), w=(), dma=False, extra=()):
        idx = len(self.ops)
        r = list(r)
        w = list(w)
        if any(isinstance(k, str) and k.startswith("o:") for k in r + w):
            r.append("OVL")
        deps = {}
        for k in r:
            if k in self.last_w:
                deps[self.last_w[k]] = True
            if isinstance(k, str) and k[0] == "p" and k[1:2].isdigit() or k in ("pS1", "pS2"):
                for q in self.readers.get(k, ()):
                    deps.setdefault(q, False)
        for k in w:
            if k in self.last_w:
                deps.setdefault(self.last_w[k], False)
            for q in self.readers.get(k, ()):
                deps.setdefault(q, False)
        for q in extra:
            deps[q] = True
        for k in r:
            self.readers.setdefault(k, []).append(idx)
        for k in w:
            self.last_w[k] = idx
            self.readers[k] = []
        deps.pop(idx, None)
        self.ops.append(dict(eng=eng, fn=fn, deps=deps, dma=dma, idx=idx))
        return idx

    def _needs_sync(self, o, D, raw):
        if D["dma"] or o["dma"]:
            return True
        if D["eng"] != o["eng"]:
            return True
        if o["eng"] == "pe":
            return False
        return raw

    def emit(self):
        nc = self.nc
        ops = self.ops
        need_inc = [False] * len(ops)
        for o in ops:
            best = {}
            nd = {}
            for d, raw in o["deps"].items():
                Dd = ops[d]
                if Dd["dma"]:
                    nd[d] = raw
                    continue
                if self._needs_sync(o, Dd, raw):
                    if best.get(Dd["eng"], -1) < d:
                        best[Dd["eng"]] = d
            for e, d in best.items():
                nd[d] = True
                need_inc[d] = True
            o["deps"] = nd
        cnt = {e: 0 for e in self.ENGS}
        incval = [0] * len(ops)
        for o in ops:
            if o["dma"]:
                continue
            if need_inc[o["idx"]]:
                cnt[o["eng"]] += 1
            incval[o["idx"]] = cnt[o["eng"]]
        dma_count = {e: 0 for e in self.ENGS}
        dma_sem = {}
        for o in ops:
            if o["dma"]:
                q = o["eng"]
                j = dma_count[q]
                dma_count[q] += 1
                dma_sem[o["idx"]] = (q, j % N_DMA_SEMS, 16 * (j // N_DMA_SEMS + 1))
        with contextlib.ExitStack() as es:
            esem = {e: es.enter_context(nc.semaphore("s_" + e)) for e in ["pe", "act", "dve", "pool"]}
            dsem = {}
            for q in self.ENGS:
                if dma_count[q] > 0:
                    dsem[q] = [es.enter_context(nc.semaphore("d_%s_%d" % (q, i)))
                               for i in range(min(N_DMA_SEMS, dma_count[q]))]
            block = es.enter_context(nc.Block())
            per_eng = {e: [o for o in ops if o["eng"] == e] for e in self.ENGS}

            def run_engine(ename, eng):
                waited = {}

                def wait(sem, val):
                    if waited.get(sem.num, 0) >= val:
                        return
                    waited[sem.num] = val
                    eng.wait_ge(sem, val)

                for o in per_eng[ename]:
                    for d in sorted(o["deps"]):
                        Dd = ops[d]
                        if Dd["dma"]:
                            q, slot, val = dma_sem[d]
                            wait(dsem[q][slot], val)
                        else:
                            wait(esem[Dd["eng"]], incval[d])
                    if o["dma"]:
                        q, slot, val = dma_sem[o["idx"]]
                        if val > 16:
                            wait(dsem[q][slot], val - 16)
                        o["fn"](eng).then_inc(dsem[q][slot], 16)
                    else:
                        ins = o["fn"](eng)
                        if need_inc[o["idx"]]:
                            ins.then_inc(esem[ename], 1)
                if ename == "sp":
                    for q in dsem:
                        n = dma_count[q]
                        for slot in range(len(dsem[q])):
                            uses = (n - slot + N_DMA_SEMS - 1) // N_DMA_SEMS
                            if uses > 0:
                                eng.wait_ge(dsem[q][slot], 16 * uses)

            @block.sync
            def _(e):
                run_engine("sp", e)

            @block.tensor
            def _(e):
                run_engine("pe", e)

            @block.scalar
            def _(e):
                run_engine("act", e)

            @block.vector
            def _(e):
                run_engine("dve", e)

            @block.gpsimd
            def _(e):
                run_engine("pool", e)


def host_consts():
    c = {}
    c["c_ident"] = np.eye(128, dtype=np.float32)
    half = 64
    inv = (10000.0 ** (-np.arange(half, dtype=np.float32) / half)).astype(np.float32)
    pos = np.concatenate([np.arange(2048), PAST + np.arange(32)]).astype(np.float32)
    ang = (pos[None, :] * inv[:, None]).astype(np.float32)
    cos = np.cos(ang).astype(np.float32)
    sin = np.sin(ang).astype(np.float32)
    c["c_cos"] = np.ascontiguousarray(np.concatenate([cos, cos], 0))
    c["c_sin"] = np.ascontiguousarray(np.concatenate([sin, -sin], 0))
    log_g = np.log1p(-np.exp2(-5.0 - np.arange(4, dtype=np.float64)))
    gam = {}
    rt = np.zeros((128, 2, 2, 4, 128), np.float32)
    kd = np.zeros((128, 2, 4), np.float32)
    for ki, cc in enumerate((128, 32)):
        idx = np.arange(cc, dtype=np.float64)
        diff = idx[None, :] - idx[:, None]
        dmT = np.where(diff >= 0, np.exp(log_g[:, None, None] * np.maximum(diff, 0.0)), 0.0) * (128 ** -0.5)
        rt[:cc, ki, 0, :, :cc] = dmT.transpose(1, 0, 2)
        qd = np.exp(log_g[:, None] * (idx + 1.0)[None, :])
        rt[:, ki, 1, :, :cc] = qd[None]
        kd[:cc, ki, :] = np.exp(log_g[None, :] * (cc - 1.0 - idx)[:, None]) * (128 ** -0.5)
        gam[ki] = [float(np.exp(log_g[h] * cc)) for h in range(4)]
    c["c_rt"] = np.ascontiguousarray(rt.reshape(128, 16, 128))
    c["c_kd"] = kd
    c["gam"] = gam
    m = np.zeros((128, 3, 128), np.float32)
    p = np.arange(128)
    m[:, 0, :] = (p[:, None] // 64 <= p[None, :] // 64)
    i64 = np.arange(64)
    m[:64, 1, :64] = (i64[:, None] > i64[None, :])
    m[:64, 2, :64] = (i64[None, :] >= i64[:, None])
    c["c_mask"] = m
    return c


NFM = 184
NROW = 1536


class Builder:
    def __init__(self, cfg):
        self.cfg = cfg
        self.nt = cfg.get("ntiles", 4)
        self.depth = cfg.get("depth", NL)
        self.stage = cfg.get("stage", 99)
        self.groups = cfg.get("groups", "ABCD")
        self.LP = max(self.nt, 1) * 512
        self.hc = host_consts()

    def MM(self, out, lhsT, rhs, start, stop, r, w):
        self.P.add("pe", lambda e: e.matmul(out, lhsT, rhs, start=start, stop=stop), r=r, w=w)

    def ACT(self, out, in_, func, r, w, bias=None, scale=None):
        kw = {}
        if bias is not None:
            kw["bias"] = bias
        if scale is not None:
            kw["scale"] = scale
        self.P.add("act", lambda e: e.activation(out=out, in_=in_, func=func, **kw), r=r, w=w)

    def TS(self, out, in0, s1, s2, op0, op1, r, w, eng="dve"):
        if s2 is None:
            self.P.add(eng, lambda e: e.tensor_scalar(out=out, in0=in0, scalar1=s1, scalar2=None, op0=op0), r=r, w=w)
        else:
            self.P.add(eng, lambda e: e.tensor_scalar(out=out, in0=in0, scalar1=s1, scalar2=s2, op0=op0, op1=op1), r=r, w=w)

    def TT(self, out, in0, in1, op, r, w, eng="dve"):
        self.P.add(eng, lambda e: e.tensor_tensor(out=out, in0=in0, in1=in1, op=op), r=r, w=w)

    def STT(self, out, in0, scalar, in1, op0, op1, r, w, eng="dve"):
        self.P.add(eng, lambda e: e.scalar_tensor_tensor(out=out, in0=in0, scalar=scalar, in1=in1, op0=op0, op1=op1), r=r, w=w)

    def CP(self, out, in_, r, w, eng="dve"):
        if eng == "act":
            self.P.add("act", lambda e: e.activation(out=out, in_=in_, func=AF.Copy), r=r, w=w)
        else:
            self.P.add(eng, lambda e: e.tensor_copy(out=out, in_=in_), r=r, w=w)

    def RECIP(self, out, in_, r, w):
        self.P.add("dve", lambda e: e.reciprocal(out=out, in_=in_), r=r, w=w)

    def RSUM(self, out, in_, r, w):
        self.P.add("dve", lambda e: e.reduce_sum(out=out, in_=in_, axis=AX.X), r=r, w=w)

    def SCAN(self, out, d0, d1, init, r, w):
        self.P.add("dve", lambda e: e.tensor_tensor_scan(out=out, data0=d0, data1=d1, initial=init, op0=ALU.mult, op1=ALU.add), r=r, w=w)

    def rsqrt(self, out, in_, eps, r, w, tmp, tmpk, scale=1.0):
        self.TS(tmp, in_, scale, eps, ALU.mult, ALU.add, r=r, w=[tmpk])
        self.ACT(tmp, tmp, AF.Sqrt, r=[tmpk], w=[tmpk])
        self.RECIP(out, tmp, r=[tmpk], w=w)

    def MEMSET(self, ap, val, w, eng="dve"):
        self.P.add(eng, lambda e: e.memset(ap, val), w=w)

    def DMA(self, q, out, in_, r, w, extra=()):
        return self.P.add(q, lambda e: e.dma_start(out=out, in_=in_), r=r, w=w, dma=True, extra=extra)

    def ps(self):
        i = self._ps_i
        self._ps_i = (i + 1) % self.NRR
        return self.psb[i], "p%d" % i

    def barrier(self):
        self.P.add("dve", lambda e: e.memset(self.bar[:], 0.0), w=["OVL", "bar"])

    def mt_reset(self, base=16384):
        self._mt = base
        self._mtn = getattr(self, "_mtn", 0)

    def mt(self, shape, dt, name=None):
        n = 1
        for d in shape[1:]:
            n *= d
        esz = 4 if dt == F32 else 2
        off = (self._mt + 31) // 32 * 32
        self._mt = off + n * esz
        assert self._mt <= self.OVL_BYTES, ("overlay overflow", self._mt)
        base = self.ovl32 if dt == F32 else self.ovl
        ap = base[0:shape[0], off // esz: off // esz + n]
        if len(shape) == 3:
            ap = ap.rearrange("p (a b) -> p a b", a=shape[1])
        elif len(shape) == 4:
            ap = ap.rearrange("p (a b c) -> p a b c", a=shape[1], b=shape[2])
        self._mtn += 1
        return ap, "o:t%d" % self._mtn

    def build(self):
        nc = bass.Bass("TRN2", target_bir_lowering=False)
        self.nc = nc
        self.P = Prog(nc)
        LP = self.LP

        def din(name, shape):
            return nc.dram_tensor(name, list(shape), F32, kind="ExternalInput").ap()

        def dout(name, shape):
            return nc.dram_tensor(name, list(shape), F32, kind="ExternalOutput").ap()

        self.xp = din("xp", [LP, D])
        self.xs = din("xs", [64, D])
        self.W = {}
        wshapes = dict(w1i=[D, 2 * DFF], w1o=[DFF, D], wmi=[D, DIN], wmo=[D, D], w2i=[D, 2 * DFF], w2o=[DFF, D])
        self.wshapes = wshapes
        for nm, sh in wshapes.items():
            self.W[nm] = din(nm, [NL] + sh)
        self.S = {nm: nc.dram_tensor("s_" + nm, [NL] + sh, BF16, kind="Internal").ap() for nm, sh in wshapes.items()}
        self.pp_fm = din("pp_fm", [NL, 128, NFM])
        self.pp_row = din("pp_row", [NL, 128, NROW])
        self.pp_bd = din("pp_bd", [NL, 128, 8, 128])
        self.pp_sgw = din("pp_sgw", [NL, 128, 4, 128])
        self.pp_sgws = din("pp_sgws", [NL, 32, 4, 32])
        self.pp_dn4 = din("pp_dn4", [NL, 4, 2])
        self.c_ident = din("c_ident", [128, 128])
        self.c_cos = din("c_cos", [128, 2080])
        self.c_sin = din("c_sin", [128, 2080])
        self.c_rt = din("c_rt", [128, 16, 128])
        self.c_kd = din("c_kd", [128, 2, 4])
        self.c_mask = din("c_mask", [128, 3, 128])
        self.st_ret = din("st_ret", [NL, 2, 4, 128, 128])
        self.st_dn = din("st_dn", [NL, 2, 4, 128, 128])
        self.st_lruh = din("st_lruh", [NL, 2, 128, 4])
        self.st_lruconv = din("st_lruconv", [NL, 2, 128, 4, 3])
        self.st_dnconv = din("st_dnconv", [NL, 2, 128, 12, 3])
        self.yp = dout("yp", [LP, D])
        self.ys = dout("ys", [64, D])
        self.o_ret = dout("o_ret", [NL, 3, 4, 128, 128])
        self.o_dn = dout("o_dn", [NL, 3, 4, 128, 128])
        self.o_lruh = dout("o_lruh", [NL, 3, 128, 4])
        self.o_lruconv = dout("o_lruconv", [NL, 3, 128, 4, 3])
        self.o_dnconv = dout("o_dnconv", [NL, 3, 128, 12, 3])
        self.o_sgv = dout("o_sgv", [NL, 2, 32, 512])
        if self.cfg.get("dbg"):
            self.dbg_mg = dout("dbg_mg", [128, KC, 512])
            self.dbg_q = [nc.dram_tensor("dbg_q%d" % i, [128, 12, 512], BF16, kind="ExternalOutput").ap() for i in range(2)]

        with contextlib.ExitStack() as es:
            self.es = es

            def sb(name, shape, dt):
                return es.enter_context(nc.sbuf_tensor(name, list(shape), dt))

            self.sb = sb
            T = 512
            self.xres = sb("xres", [128, KC, T], F32)
            self.xbf = sb("xbf", [128, KC, T], BF16)
            self.wring = [sb("wr%d" % i, [128, KC, 128], BF16) for i in range(4)]
            self.woring = [sb("wo%d" % i, [128, HC, 128], BF16) for i in range(2)]
            self.OVL_BYTES = 61440
            self.ovl = sb("ovl", [128, self.OVL_BYTES // 2], BF16)
            self.ovl32 = self.ovl.bitcast(F32)
            cb = 45056

            def c32(i):
                return self.ovl32[:, (cb + i * 2048) // 4:(cb + (i + 1) * 2048) // 4]

            def c16(i):
                return self.ovl[:, (cb + 8192 + i * 1024) // 2:(cb + 8192 + (i + 1) * 1024) // 2]

            self.tA = [c32(0), c32(1)]
            self.sil = [c32(2), c32(3)]
            self.ybf = [c16(0), c16(1)]
            self.ysq = [c16(2), c16(3)]
            self.MT_TOP_A = 45056
            self.lnm = sb("lnm", [128, T], F32)
            self.lnr = sb("lnr", [128, T], F32)
            self.lnn = sb("lnn", [128, T], F32)
            self.ident32 = sb("ident32", [128, 128], F32)
            self.identb = sb("identb", [128, 128], BF16)
            self.onesb = sb("onesb", [128, 128], BF16)
            self.ones4 = sb("ones4", [4, 128], F32)
            self.bar = sb("bar", [128, 1], F32)
            self.ppfm = sb("ppfm", [128, NL, NFM], F32)
            self.pprow = sb("pprow", [128, NROW], F32)
            self.bd32 = None
            self.bdb = sb("bdb", [128, NL * 8, 128], BF16)
            self.sgwb = sb("sgwb", [128, NL, 4, 128], BF16)
            self.sgwsb = sb("sgwsb", [32, NL, 4, 32], BF16)
            self.dn4 = sb("dn4", [4, NL, 4], F32)
            self.lrup = sb("lrup", [128, NL, 4, 2], F32)
            self.cosb = sb("cosb", [128, T], F32)
            self.sinb = sb("sinb", [128, T], F32)
            self.rt = sb("rt", [128, 16, 128], F32)
            self.kd = sb("kd", [128, 2, 4], F32)
            self.mask = sb("mask", [128, 3, 128], F32)
            self.Sret = sb("Sret", [128, NL * 2, 512], F32)
            self.Sdn = sb("Sdn", [128, NL * 2, 512], F32)
            self.hst = sb("hst", [128, NL * 2, 4], F32)
            self.lct = sb("lct", [128, NL * 2 * 4, 3], F32)
            self.dct = sb("dct", [128, NL * 2 * 12, 3], F32)
            self.psb = [es.enter_context(nc.psum_tensor("pb%d" % i, [128, 512], F32)) for i in range(8)]
            self.NRR = 6
            self._ps_i = 0
            self.pS1, self.pS2 = self.psb[6], self.psb[7]
            print("sbuf bytes remaining", nc.sbuf_bytes_remaining)

            self.setup()
            self.casts()
            for ti in range(self.nt):
                self.run_tile("p", ti)
            if self.nt > 0:
                self.store_states("p")
            if self.cfg.get("sample", True):
                self.load_states_sample()
                self.run_tile("s", 0)
                self.store_states("s")
            self.P.emit()
        return nc

    def setup(self):
        self.DMA("sp", self.ident32[:], self.c_ident, r=[], w=["ident32"])
        self.CP(self.identb[:], self.ident32[:], r=["ident32"], w=["identb"])
        self.MEMSET(self.onesb[:], 1.0, w=["onesb"])
        self.MEMSET(self.ones4[:], 1.0, w=["ones4"])
        for l in range(NL):
            self.DMA("sp", self.ppfm[:, l, :], self.pp_fm[l], r=[], w=["ppfm"])
        self.DMA("sp", self.rt[:], self.c_rt, r=[], w=["rt"])
        self.DMA("sp", self.kd[:], self.c_kd, r=[], w=["kd"])
        self.DMA("sp", self.mask[:], self.c_mask, r=[], w=["mask"])
        for l in range(NL):
            self.DMA("sp", self.dn4[:, l, 0:2], self.pp_dn4[l], r=[], w=["dn4"])
        self.ACT(self.dn4[:, :, 2], self.dn4[:, :, 0], AF.Exp, r=["dn4"], w=["dn4"])
        self.TS(self.dn4[:, :, 2], self.dn4[:, :, 2], -1.0, None, ALU.mult, None, r=["dn4"], w=["dn4"])
        self.barrier()
        self.mt_reset(0)
        for l in range(NL):
            t, tk = self.mt([128, 8, 128], F32)
            self.DMA("sp", t, self.pp_bd[l], r=[], w=[tk])
            self.CP(self.bdb[:, l * 8:(l + 1) * 8, :], t, r=[tk], w=["bdb"])
            t2, t2k = self.mt([128, 4, 128], F32)
            self.DMA("sp", t2, self.pp_sgw[l], r=[], w=[t2k])
            for h in range(4):
                self.TT(self.sgwb[:, l, h, :], t2[:, h, :], self.mask[:, 0, :], ALU.mult, r=[t2k, "mask"], w=["sgwb"])
            t3, t3k = self.mt([32, 4, 32], F32)
            self.DMA("sp", t3, self.pp_sgws[l], r=[], w=[t3k])
            self.CP(self.sgwsb[:, l], t3, r=[t3k], w=["sgwsb"])
        for l in range(NL):
            lam = self.ppfm[:, l, 124:128]
            t, tk = self.mt([128, 4], F32)
            self.ACT(t, lam, AF.Exp, r=["ppfm"], w=[tk], scale=-1.0)
            self.ACT(t, t, AF.Ln, r=[tk], w=[tk], bias=1.0)
            self.TS(self.lrup[:, l, :, 0], t, -8.0, None, ALU.mult, None, r=[tk], w=["lrup"])
            self.TS(self.lrup[:, l, :, 1], t, -16.0, None, ALU.mult, None, r=[tk], w=["lrup"])
        for nm in ["Sret", "Sdn", "hst", "lct", "dct"]:
            self.MEMSET(getattr(self, nm)[:], 0.0, w=[nm])

    CBLK = dict(w1i=1408, w2i=1408, w1o=512, w2o=512, wmi=1024, wmo=512)

    def casts(self):
        self.cast_ops = {}
        order = ["w1i", "w1o", "wmi", "wmo", "w2i", "w2o"]
        for l in range(self.depth):
            for nm in order:
                R, C = self.wshapes[nm]
                cb = self.CBLK[nm]
                nblk = (C + cb - 1) // cb
                if nm in ("w1i", "w2i"):
                    blks = [0, 4, 1, 5, 2, 6, 3, 7]
                else:
                    blks = list(range(nblk))
                for bi in blks:
                    ids = []
                    c0, c1 = bi * cb, min(C, (bi + 1) * cb)
                    if not self.cfg.get("nocast"):
                        for rb in range(R // 128):
                            ids.append(self.DMA("pool", self.S[nm][l, rb * 128:(rb + 1) * 128, c0:c1],
                                                self.W[nm][l, rb * 128:(rb + 1) * 128, c0:c1], r=[], w=[]))
                    self.cast_ops[(nm, l, bi)] = ids

    def cast_deps(self, nm, l, col0, ncols):
        cb = self.CBLK[nm]
        out = []
        for bi in range(col0 // cb, (col0 + ncols - 1) // cb + 1):
            out += self.cast_ops[(nm, l, bi)]
        return out

    def load_panel(self, nm, l, col0, ncols, slot):
        src = self.S[nm][l, :, col0:col0 + ncols].rearrange("(kc p) n -> p kc n", p=128)
        self.DMA("sp", self.wring[slot][:, :, 0:ncols], src, r=[], w=["wr%d" % slot], extra=self.cast_deps(nm, l, col0, ncols))

    def load_wo(self, nm, l, m, slot):
        src = self.S[nm][l, :, m * 128:(m + 1) * 128].rearrange("(fc p) n -> p fc n", p=128)
        self.DMA("sp", self.woring[slot][:], src, r=[], w=["wo%d" % slot], extra=self.cast_deps(nm, l, m * 128, 128))

    def pj_begin(self, order):
        self.pj_order = order
        self.pj_issued = 0
        self.pj_pos = -1

    def pj_next(self):
        self.pj_pos += 1
        while self.pj_issued < len(self.pj_order) and self.pj_issued <= self.pj_pos + 2:
            nm, l, col0, ncols = self.pj_order[self.pj_issued]
            self.load_panel(nm, l, col0, ncols, self.pj_issued % 4)
            self.pj_issued += 1
        return self.pj_pos % 4

    def proj(self, ncols=128):
        T = self.T
        slot = self.pj_next()
        pb, pk = self.ps()
        for kc in range(KC):
            self.MM(pb[0:ncols, 0:T], self.wring[slot][:, kc, 0:ncols], self.xbf[:, kc, 0:T], kc == 0, kc == KC - 1,
                    r=["wr%d" % slot, "xbf%d" % kc], w=[pk])
        return pb, pk

    def run_tile(self, kind, ti):
        T = 512 if kind == "p" else 64
        self.T = T
        self.kind = kind
        self.ki = 0 if kind == "p" else 1
        self.ti = ti
        self.segs = [(0, 512, 0)] if kind == "p" else [(0, 32, 0), (32, 32, 1)]
        if not self.cfg.get("noload"):
            self.load_x(kind, ti)
        if kind == "p":
            self.DMA("sp", self.cosb[:, 0:T], self.c_cos[:, ti * 512:(ti + 1) * 512], r=[], w=["cosb"])
            self.DMA("sp", self.sinb[:, 0:T], self.c_sin[:, ti * 512:(ti + 1) * 512], r=[], w=["sinb"])
        else:
            for s in range(2):
                self.DMA("sp", self.cosb[:, s * 32:(s + 1) * 32], self.c_cos[:, 2048:2080], r=[], w=["cosb"])
                self.DMA("sp", self.sinb[:, s * 32:(s + 1) * 32], self.c_sin[:, 2048:2080], r=[], w=["sinb"])
        for l in range(self.depth):
            if self.stage >= 1:
                self.ffn(l, "w1i", "w1o", 0)
            if self.stage >= 2:
                self.mixer(l)
            if self.stage >= 3:
                self.ffn(l, "w2i", "w2o", 4)
        if not self.cfg.get("nostore"):
            self.store_x(kind, ti)

    def load_x(self, kind, ti):
        T = self.T
        nb = (T + 127) // 128
        self.barrier()
        for b in range(nb):
            rows = min(128, T - b * 128)
            io = self.ovl32[0:rows, b % 2 * D:(b % 2 + 1) * D]
            iok = "o:io%d" % (b % 2)
            src = (self.xp[ti * 512 + b * 128: ti * 512 + b * 128 + rows, :] if kind == "p" else self.xs[0:rows, :])
            self.DMA("sp", io, src, r=[], w=[iok])
            for g in range(4):
                pb, pk = self.ps()
                for q in range(4):
                    m = g * 4 + q
                    self.MM(pb[:, q * 128:q * 128 + rows], io[:, m * 128:(m + 1) * 128], self.ident32[0:rows, 0:rows],
                            True, True, r=[iok, "ident32"], w=[pk])
                pv = pb[:].rearrange("p (q t) -> p q t", q=4)[:, :, 0:rows]
                self.CP(self.xres[:, g * 4:(g + 1) * 4, b * 128:b * 128 + rows], pv, r=[pk],
                        w=["xres%d" % (g * 4 + q) for q in range(4)])
                for q in range(4):
                    self.CP(self.xbf[:, g * 4 + q, b * 128:b * 128 + rows], self.xres[:, g * 4 + q, b * 128:b * 128 + rows],
                            r=["xres%d" % (g * 4 + q)], w=["xbf%d" % (g * 4 + q)], eng="act")

    def store_x(self, kind, ti):
        T = self.T
        nb = (T + 127) // 128
        self.barrier()
        for b in range(nb):
            rows = min(128, T - b * 128)
            io = self.ovl32[0:rows, b % 2 * D:(b % 2 + 1) * D]
            iok = "o:io%d" % (b % 2)
            for g in range(4):
                pb, pk = self.ps()
                for q in range(4):
                    m = g * 4 + q
                    self.MM(pb[0:rows, q * 128:(q + 1) * 128], self.xres[:, m, b * 128:b * 128 + rows], self.ident32[:],
                            True, True, r=["xres%d" % m, "ident32"], w=[pk])
                self.CP(io[:, g * 512:(g + 1) * 512], pb[0:rows, :], r=[pk], w=[iok + "_%d" % g], eng=("act" if g % 2 else "dve"))
            dst = (self.yp[ti * 512 + b * 128: ti * 512 + b * 128 + rows, :] if kind == "p" else self.ys[0:rows, :])
            self.DMA("sp", dst, io, r=[iok + "_%d" % g for g in range(4)], w=[iok + "_%d" % g for g in range(4)] + [iok])

    def load_states_sample(self):
        for l in range(NL):
            for s in range(2):
                q = l * 2 + s
                self.DMA("sp", self.Sret[:, q, :].rearrange("p (h e) -> p h e", h=4), self.st_ret[l, s].rearrange("h k v -> k h v"), r=[], w=["Sret"])
                self.DMA("sp", self.Sdn[:, q, :].rearrange("p (h e) -> p h e", h=4), self.st_dn[l, s].rearrange("h k v -> k h v"), r=[], w=["Sdn"])
                self.DMA("sp", self.hst[:, q, :], self.st_lruh[l, s], r=[], w=["hst"])
                self.DMA("sp", self.lct[:, q * 4:(q + 1) * 4, :], self.st_lruconv[l, s], r=[], w=["lct"])
                self.DMA("sp", self.dct[:, q * 12:(q + 1) * 12, :], self.st_dnconv[l, s], r=[], w=["dct"])

    def store_states(self, kind):
        for l in range(NL):
            for s in ([0] if kind == "p" else [0, 1]):
                o = 0 if kind == "p" else 1 + s
                q = l * 2 + s
                self.DMA("sp", self.o_ret[l, o].rearrange("h k v -> k h v"), self.Sret[:, q, :].rearrange("p (h e) -> p h e", h=4), r=["Sret"], w=[])
                self.DMA("sp", self.o_dn[l, o].rearrange("h k v -> k h v"), self.Sdn[:, q, :].rearrange("p (h e) -> p h e", h=4), r=["Sdn"], w=[])
                self.DMA("sp", self.o_lruh[l, o], self.hst[:, q, :], r=["hst"], w=[])
                self.DMA("sp", self.o_lruconv[l, o], self.lct[:, q * 4:(q + 1) * 4, :], r=["lct"], w=[])
                self.DMA("sp", self.o_dnconv[l, o], self.dct[:, q * 12:(q + 1) * 12, :], r=["dct"], w=[])

    def ln_stats_chunk(self, m):
        T = self.T
        i = m % 2
        yk, qk = "o:ybf%d" % i, "o:ysq%d" % i
        self.ACT(self.ybf[i][:, 0:T], self.xres[:, m, 0:T], AF.Copy, r=["xres%d" % m], w=[yk])
        self.ACT(self.ysq[i][:, 0:T], self.xres[:, m, 0:T], AF.Square, r=["xres%d" % m], w=[qk])
        self.MM(self.pS1[:, 0:T], self.onesb[:], self.ybf[i][:, 0:T], m == 0, m == KC - 1, r=["onesb", yk], w=["pS1"])
        self.MM(self.pS2[:, 0:T], self.onesb[:], self.ysq[i][:, 0:T], m == 0, m == KC - 1, r=["onesb", qk], w=["pS2"])

    def ln_finish(self, l, which, eps):
        T = self.T
        mean, rstd, nmr = self.lnm[:, 0:T], self.lnr[:, 0:T], self.lnn[:, 0:T]
        t0 = self.tA[0][:, 0:T]
        self.ACT(mean, self.pS1[:, 0:T], AF.Copy, r=["pS1"], w=["lnm"], scale=1.0 / D)
        self.TT(t0, mean, mean, ALU.mult, r=["lnm"], w=["o:tA0"])
        self.STT(t0, self.pS2[:, 0:T], 1.0 / D, t0, ALU.mult, ALU.subtract, r=["pS2", "o:tA0"], w=["o:tA0"])
        self.rsqrt(rstd, t0, eps, r=["o:tA0"], w=["lnr"], tmp=self.tA[1][:, 0:T], tmpk="o:tA1")
        self.STT(nmr, mean, -1.0, rstd, ALU.mult, ALU.mult, r=["lnm", "lnr"], w=["lnn"])
        for m in range(KC):
            i = m % 2
            t = self.tA[i][:, 0:T]
            tk = "o:tA%d" % i
            self.TT(t, self.xres[:, m, 0:T], rstd, ALU.mult, r=["xres%d" % m, "lnr"], w=[tk])
            self.TT(t, t, nmr, ALU.add, r=[tk, "lnn"], w=[tk])
            gcol = self.ppfm[:, l, which * 16 + m: which * 16 + m + 1]
            bcol = self.ppfm[:, l, (which + 1) * 16 + m: (which + 1) * 16 + m + 1]
            self.ACT(self.xres[:, m, 0:T], t, AF.Identity, r=[tk, "ppfm"], w=["xres%d" % m], bias=bcol, scale=gcol)
            self.ACT(self.xbf[:, m, 0:T], self.xres[:, m, 0:T], AF.Copy, r=["xres%d" % m], w=["xbf%d" % m])

    def ffn(self, l, wi, wo, which):
        T = self.T
        self.barrier()
        hb = self.ovl[:, 0:HC * 512].rearrange("p (j t) -> p j t", j=HC)
        order = []
        for j in range(HC):
            order.append((wi, l, j * 128, 128))
            order.append((wi, l, DFF + j * 128, 128))
        self.pj_begin(order)
        for j in range(HC):
            sg = self.pj_next()
            su = self.pj_next()
            if j == HC - 2:
                self.load_wo(wo, l, 0, 0)
                self.load_wo(wo, l, 1, 1)
            pg, pgk = self.ps()
            pu, puk = self.ps()
            for kc in range(KC):
                self.MM(pg[:, 0:T], self.wring[sg][:, kc, :], self.xbf[:, kc, 0:T],
                        kc == 0, kc == KC - 1, r=["wr%d" % sg, "xbf%d" % kc], w=[pgk])
            for kc in range(KC):
                self.MM(pu[:, 0:T], self.wring[su][:, kc, :], self.xbf[:, kc, 0:T],
                        kc == 0, kc == KC - 1, r=["wr%d" % su, "xbf%d" % kc], w=[puk])
            si = j % 2
            sk = "o:sil%d" % si
            self.ACT(self.sil[si][:, 0:T], pg[:, 0:T], AF.Silu, r=[pgk], w=[sk])
            self.TT(hb[:, j, 0:T], self.sil[si][:, 0:T], pu[:, 0:T], ALU.mult, r=[sk, puk], w=["o:h%d" % j])
        for m in range(KC):
            s = m % 2
            po, pok = self.ps()
            for fc in range(HC):
                self.MM(po[:, 0:T], self.woring[s][:, fc, :], hb[:, fc, 0:T], fc == 0, fc == HC - 1,
                        r=["wo%d" % s, "o:h%d" % fc], w=[pok])
            if m + 2 < KC:
                self.load_wo(wo, l, m + 2, s)
            self.STT(self.xres[:, m, 0:T], self.xres[:, m, 0:T], 2.0 * ALPHA, po[:, 0:T], ALU.mult, ALU.add,
                     r=["xres%d" % m, pok], w=["xres%d" % m])
            self.ln_stats_chunk(m)
        self.ln_finish(l, which, 4.0 * EPS)

    def gelu(self, dst, src, tmp, r, w, tk):
        self.TT(tmp, src, src, ALU.mult, r=r, w=[tk])
        self.TS(tmp, tmp, 0.044715, 1.0, ALU.mult, ALU.add, r=[tk], w=[tk])
        self.TT(tmp, tmp, src, ALU.mult, r=[tk] + r, w=[tk])
        self.ACT(tmp, tmp, AF.Sigmoid, r=[tk], w=[tk], scale=1.5957691216057308)
        self.TT(dst, src, tmp, ALU.mult, r=[tk] + r, w=w)

    def mixer(self, l):
        T = self.T
        self.barrier()
        self.mg = self.ovl[:, 0:KC * 512].rearrange("p (m t) -> p m t", m=KC)
        self.DMA("sp", self.pprow[:], self.pp_row[l], r=[], w=["pprow"])
        order = []

        def cols(c0, n):
            for i in range(n):
                order.append(("wmi", l, c0 + i * 128, 128))
        if "A" in self.groups:
            cols(0, 16)
        if "B" in self.groups:
            for c in range(4):
                order.append(("wmi", l, 2560 + c * 128, 128))
                order.append(("wmi", l, 2048 + c * 128, 128))
        if "C" in self.groups:
            cols(3584, 4)
            cols(3072, 4)
        if "D" in self.groups:
            order.append(("wmi", l, 6144, 8))
            cols(4096, 16)
        for m in range(KC):
            order.append(("wmo", l, m * 128, 128))
        self.pj_begin(order)
        for gi, g in enumerate("ABCD"):
            if g in self.groups:
                getattr(self, "grp" + g)(l)
            else:
                for c in range(4):
                    self.MEMSET(self.mg[:, gi * 4 + c, 0:T], 0.0, w=["o:mg%d" % (gi * 4 + c)])
            self.barrier()
        if self.cfg.get("dbg") and self.kind == "p" and l == self.depth - 1:
            self.barrier()
            t, tk = (self.ovl32[:, 16384 // 4:(16384 + KC * 512 * 4) // 4].rearrange("p (m t) -> p m t", m=KC), "o:dbg")
            for m in range(KC):
                self.CP(t[:, m, :], self.mg[:, m, :], r=["o:mg%d" % m], w=[tk])
            self.DMA("sp", self.dbg_mg, t, r=[tk], w=[])
            self.barrier()
        for m in range(KC):
            slot = self.pj_next()
            po, pok = self.ps()
            for kc in range(KC):
                self.MM(po[:, 0:T], self.wring[slot][:, kc, :], self.mg[:, kc, 0:T], kc == 0, kc == KC - 1,
                        r=["wr%d" % slot, "o:mg%d" % kc], w=[pok])
            self.STT(self.xres[:, m, 0:T], self.xres[:, m, 0:T], ALPHA, po[:, 0:T], ALU.mult, ALU.add,
                     r=["xres%d" % m, pok], w=["xres%d" % m])
            self.ln_stats_chunk(m)
        self.ln_finish(l, 2, EPS)

    def grpA(self, l):
        T = self.T
        ki = self.ki
        self.mt_reset()
        qrot, qrk = self.mt([128, 4, T], BF16)
        qdec, qdk = self.mt([128, 4, T], BF16)
        krot, krk = self.mt([128, 4, T], BF16)
        vT, vTk = self.mt([128, 4, T], BF16)
        gate, gk = self.mt([128, 4, T], BF16)
        X, Xk = self.mt([128, T], F32)
        t1, t1k = self.mt([128, T], F32)
        t2, t2k = self.mt([128, T], F32)
        cq = 128 if self.kind == "p" else 32
        for which in range(2):
            for h in range(4):
                pb, pk = self.proj()
                self.CP(X, pb[:, 0:T], r=[pk], w=[Xk], eng="act")
                self.TT(t1, X, self.cosb[:, 0:T], ALU.mult, r=[Xk, "cosb"], w=[t1k])
                self.TT(t2[0:64], X[64:128], self.sinb[64:128, 0:T], ALU.mult, r=[Xk, "sinb"], w=[t2k + "a"])
                self.TT(t2[64:128], X[0:64], self.sinb[0:64, 0:T], ALU.mult, r=[Xk, "sinb"], w=[t2k + "b"])
                self.TT(t1, t1, t2, ALU.add, r=[t1k, t2k + "a", t2k + "b"], w=[t1k])
                if which == 0:
                    self.CP(qrot[:, h, :], t1, r=[t1k], w=[qrk], eng="act")
                    for (c0, L, slot) in self.segs:
                        for ci in range(L // cq):
                            a = c0 + ci * cq
                            self.TT(qdec[:, h, a:a + cq], t1[:, a:a + cq], self.rt[:, ki * 8 + 4 + h, 0:cq], ALU.mult,
                                    r=[t1k, "rt"], w=[qdk])
                else:
                    self.CP(krot[:, h, :], t1, r=[t1k], w=[krk], eng="act")
        for h in range(4):
            pb, pk = self.proj()
            self.CP(vT[:, h, :], pb[:, 0:T], r=[pk], w=[vTk], eng="act")
        for h in range(4):
            pb, pk = self.proj()
            self.ACT(gate[:, h, :], pb[:, 0:T], AF.Silu, r=[pk], w=[gk])
        Sbf, Sbk = self.mt([128, 4, 128], BF16)
        kTM, kTk = self.mt([128, 4, 128], BF16)
        vTM, vMk = self.mt([128, 512], BF16)
        scb, sck = self.mt([128, 4, 128], BF16)
        o32, o3k = self.mt([128, 4, 128], F32)
        sq, sqk = self.mt([128, 4, 128], F32)
        st, stk = self.mt([128, 16], F32)
        gam = self.hc["gam"][ki]
        NCA = self.cfg.get("a_chains", 2)
        HPA = 4 // NCA
        vTM3 = vTM.rearrange("p (h e) -> p h e", h=4)

        def chainA(ch, a, c, S32, first):
            hs = list(range(ch * HPA, (ch + 1) * HPA))
            h0, h1 = hs[0], hs[-1] + 1
            nh = len(hs)
            sfx = "_a%d" % ch

            def K(k):
                return k + sfx
            SK = "Sret" + sfx
            pb, pk = self.ps()
            for h in hs:
                hi = h - h0
                self.MM(pb[0:c, hi * 128:(hi + 1) * 128], krot[:, h, a:a + c], self.identb[:], True, True,
                        r=[krk, "identb"], w=[pk])
                self.MM(pb[0:c, 256 + hi * 128:256 + (hi + 1) * 128], vT[:, h, a:a + c], self.identb[:], True, True,
                        r=[vTk, "identb"], w=[pk])
            pb3, pk3 = self.ps()
            for h in hs:
                hi = h - h0
                self.MM(pb3[0:c, hi * 128:hi * 128 + c], krot[:, h, a:a + c], qrot[:, h, a:a + c], True, True,
                        r=[krk, qrk], w=[pk3])
            yield
            for h in hs:
                hi = h - h0
                self.TS(kTM[0:c, h, :], pb[0:c, hi * 128:(hi + 1) * 128], self.kd[0:c, ki, h:h + 1], None, ALU.mult, None,
                        r=[pk, "kd"], w=[K(kTk)])
            pbv = pb[:, 256:256 + nh * 128].rearrange("p (h e) -> p h e", h=nh)
            self.CP(vTM3[0:c, h0:h1, :], pbv[0:c], r=[pk], w=[K(vMk)], eng="act")
            p3v = pb3[:, 0:nh * 128].rearrange("p (h i) -> p h i", h=nh)
            self.TT(scb[0:c, h0:h1, 0:c], p3v[0:c, :, 0:c], self.rt[0:c, ki * 8 + h0:ki * 8 + h1, 0:c], ALU.mult,
                    r=[pk3, "rt"], w=[K(sck)])
            if first:
                self.CP(Sbf[:, h0:h1, :], S32[:, h0:h1, :], r=["Sret", SK], w=[K(Sbk)], eng="act")
            yield
            pb4, pk4 = self.ps()
            for h in hs:
                hi = h - h0
                self.MM(pb4[0:c, hi * 128:(hi + 1) * 128], scb[0:c, h, 0:c], vTM3[0:c, h, :], True, False,
                        r=[K(sck), K(vMk)], w=[pk4])
                self.MM(pb4[0:c, hi * 128:(hi + 1) * 128], qdec[:, h, a:a + c], Sbf[:, h, :], False, True,
                        r=[qdk, K(Sbk)], w=[pk4])
                self.MM(pb4[:, 256 + hi * 128:256 + (hi + 1) * 128], kTM[0:c, h, :], vTM3[0:c, h, :], True, True,
                        r=[K(kTk), K(vMk)], w=[pk4])
            yield
            p4v = pb4[:, 0:nh * 128].rearrange("p (h e) -> p h e", h=nh)
            self.CP(o32[0:c, h0:h1, :], p4v[0:c], r=[pk4], w=[K(o3k)], eng="act")
            for h in hs:
                hi = h - h0
                self.STT(S32[:, h, :], S32[:, h, :], gam[h], pb4[:, 256 + hi * 128:256 + (hi + 1) * 128], ALU.mult, ALU.add,
                         r=["Sret", SK, pk4], w=[SK])
            yield
            self.CP(Sbf[:, h0:h1, :], S32[:, h0:h1, :], r=[SK], w=[K(Sbk)], eng="act")
            self.RSUM(st[0:c, h0:h1], o32[0:c, h0:h1, :], r=[K(o3k)], w=[K(stk)])
            self.TT(sq[0:c, h0:h1, :], o32[0:c, h0:h1, :], o32[0:c, h0:h1, :], ALU.mult, r=[K(o3k)], w=[K(sqk)])
            self.RSUM(st[0:c, 4 + h0:4 + h1], sq[0:c, h0:h1, :], r=[K(sqk)], w=[K(stk)])
            yield
            self.TS(st[0:c, h0:h1], st[0:c, h0:h1], 1.0 / 128, None, ALU.mult, None, r=[K(stk)], w=[K(stk)])
            self.TT(st[0:c, 8 + h0:8 + h1], st[0:c, h0:h1], st[0:c, h0:h1], ALU.mult, r=[K(stk)], w=[K(stk)])
            self.STT(st[0:c, 4 + h0:4 + h1], st[0:c, 4 + h0:4 + h1], 1.0 / 128, st[0:c, 8 + h0:8 + h1], ALU.mult, ALU.subtract,
                     r=[K(stk)], w=[K(stk)])
            yield
            self.rsqrt(st[0:c, 4 + h0:4 + h1], st[0:c, 4 + h0:4 + h1], EPS, r=[K(stk)], w=[K(stk)], tmp=st[0:c, 12 + h0:12 + h1],
                       tmpk=K(stk))
            yield
            for h in hs:
                self.TS(o32[0:c, h, :], o32[0:c, h, :], st[0:c, h:h + 1], st[0:c, 4 + h:5 + h], ALU.subtract, ALU.mult,
                        r=[K(o3k), K(stk)], w=[K(o3k)])
            yield
            pb5, pk5 = self.ps()
            for h in hs:
                hi = h - h0
                self.MM(pb5[:, hi * 128:hi * 128 + c], o32[0:c, h, :], self.ident32[0:c, 0:c], True, True,
                        r=[K(o3k), "ident32"], w=[pk5])
            yield
            for h in hs:
                hi = h - h0
                self.STT(self.mg[:, h, a:a + c], pb5[:, hi * 128:hi * 128 + c], self.ppfm[:, l, 176 + h:177 + h],
                         gate[:, h, a:a + c], ALU.mult, ALU.mult, r=[pk5, "ppfm", gk], w=["o:mg%d" % h])

        import itertools
        for (c0, L, slot) in self.segs:
            S32 = self.Sret[:, l * 2 + slot, :].rearrange("p (h e) -> p h e", h=4)
            for ci in range(L // cq):
                a = c0 + ci * cq
                gens = [chainA(ch, a, cq, S32, ci == 0) for ch in range(NCA)]
                for _ in itertools.zip_longest(*gens):
                    pass
            self.CP(self.bar[:, 0:1], self.bar[:, 0:1], r=["Sret_a%d" % ch for ch in range(NCA)] + ["bar"], w=["Sret", "bar"])

    def grpB(self, l):
        T = self.T
        self.mt_reset()
        nseg = len(self.segs)
        L = self.segs[0][1]
        xpre, xpk = self.mt([128, nseg, L + 3], F32)
        xc, xck = self.mt([128, T], F32)
        xcb, xbk = self.mt([128, T], BF16)
        rr, rk = self.mt([128, T], F32)
        ii, ik = self.mt([128, T], F32)
        aa, ak = self.mt([128, T], F32)
        a2, a2k = self.mt([128, T], F32)
        hh, hk = self.mt([128, T], F32)
        yy, yk = self.mt([128, T], F32)
        tg, tgk = self.mt([128, T], F32)
        for c in range(4):
            pb, pk = self.proj()
            for si, (c0, L, slot) in enumerate(self.segs):
                self.CP(xpre[:, si, 0:3], self.lct[:, (l * 2 + slot) * 4 + c, :], r=["lct"], w=[xpk])
                self.CP(xpre[:, si, 3:3 + L], pb[:, c0:c0 + L], r=[pk], w=[xpk], eng="act")
                w0 = 96 + c * 4
                self.TS(xc[:, c0:c0 + L], xpre[:, si, 0:L], self.ppfm[:, l, w0:w0 + 1], self.ppfm[:, l, 112 + c:113 + c],
                        ALU.mult, ALU.add, r=[xpk, "ppfm"], w=[xck])
                for j in range(1, 4):
                    self.STT(xc[:, c0:c0 + L], xpre[:, si, j:j + L], self.ppfm[:, l, w0 + j:w0 + j + 1], xc[:, c0:c0 + L],
                             ALU.mult, ALU.add, r=[xpk, "ppfm", xck], w=[xck])
                self.CP(self.lct[:, (l * 2 + slot) * 4 + c, :], xpre[:, si, L:L + 3], r=[xpk], w=["lct"])
            self.CP(xcb, xc, r=[xck], w=[xbk], eng="act")
            pr, prk = self.ps()
            self.MM(pr[:, 0:T], self.bdb[:, l * 8 + c, :], xcb, True, True, r=["bdb", xbk], w=[prk])
            pi, pik = self.ps()
            self.MM(pi[:, 0:T], self.bdb[:, l * 8 + 4 + c, :], xcb, True, True, r=["bdb", xbk], w=[pik])
            self.ACT(rr, pr[:, 0:T], AF.Sigmoid, r=[prk, "ppfm"], w=[rk], bias=self.ppfm[:, l, 116 + c:117 + c])
            self.ACT(ii, pi[:, 0:T], AF.Sigmoid, r=[pik, "ppfm"], w=[ik], bias=self.ppfm[:, l, 120 + c:121 + c])
            self.ACT(aa, rr, AF.Exp, r=[rk, "lrup"], w=[ak], scale=self.lrup[:, l, c, 0:1])
            self.ACT(a2, rr, AF.Exp, r=[rk, "lrup"], w=[a2k], scale=self.lrup[:, l, c, 1:2])
            self.TS(a2, a2, -1.0, 1.0, ALU.mult, ALU.add, r=[a2k], w=[a2k])
            self.ACT(a2, a2, AF.Sqrt, r=[a2k], w=[a2k])
            self.TT(a2, a2, ii, ALU.mult, r=[a2k, ik], w=[a2k])
            self.TT(a2, a2, xc, ALU.mult, r=[a2k, xck], w=[a2k])
            for si, (c0, L, slot) in enumerate(self.segs):
                self.SCAN(hh[:, c0:c0 + L], aa[:, c0:c0 + L], a2[:, c0:c0 + L], self.hst[:, l * 2 + slot, c:c + 1],
                          r=[ak, a2k, "hst"], w=[hk])
                self.CP(self.hst[:, l * 2 + slot, c:c + 1], hh[:, c0 + L - 1:c0 + L], r=[hk], w=["hst"])
            pb2, pk2 = self.proj()
            self.CP(yy, pb2[:, 0:T], r=[pk2], w=[yk], eng="act")
            self.gelu(yy, yy, tg, r=[yk], w=[yk], tk=tgk)
            self.TT(self.mg[:, 4 + c, 0:T], hh, yy, ALU.mult, r=[hk, yk], w=["o:mg%d" % (4 + c)])

    def grpC(self, l):
        T = self.T
        self.mt_reset()
        gv, gvk = self.mt([128, 4, T], F32)
        sT, sTk = self.mt([128, 4, T], F32)
        yy, yk = self.mt([128, T], F32)
        tg, tgk = self.mt([128, T], F32)
        vpre, vpk = self.mt([128, 512], F32)
        vsq, vqk = self.mt([128, 512], F32)
        vnb, vbk = self.mt([128, 512], BF16)
        st, stk = self.mt([128, 8], F32)
        for c in range(4):
            pb, pk = self.proj()
            self.CP(yy, pb[:, 0:T], r=[pk], w=[yk], eng="act")
            self.gelu(gv[:, c, :], yy, tg, r=[yk], w=[gvk], tk=tgk)
        cg = 128 if self.kind == "p" else 32
        for si, (c0, L, slot) in enumerate(self.segs):
            for ci in range(L // cg):
                a = c0 + ci * cg
                c = cg
                pb, pk = self.ps()
                for q in range(4):
                    self.MM(pb[0:c, q * 128:(q + 1) * 128], gv[:, q, a:a + c], self.ident32[:], True, True,
                            r=[gvk, "ident32"], w=[pk])
                self.CP(vpre[0:c], pb[0:c, :], r=[pk], w=[vpk], eng="act")
                self.RSUM(st[0:c, 0:1], vpre[0:c], r=[vpk], w=[stk])
                self.TT(vsq[0:c], vpre[0:c], vpre[0:c], ALU.mult, r=[vpk], w=[vqk])
                self.RSUM(st[0:c, 1:2], vsq[0:c], r=[vqk], w=[stk])
                self.TS(st[0:c, 0:1], st[0:c, 0:1], 1.0 / 512, None, ALU.mult, None, r=[stk], w=[stk])
                self.TT(st[0:c, 2:3], st[0:c, 0:1], st[0:c, 0:1], ALU.mult, r=[stk], w=[stk])
                self.STT(st[0:c, 1:2], st[0:c, 1:2], 1.0 / 512, st[0:c, 2:3], ALU.mult, ALU.subtract, r=[stk], w=[stk])
                self.rsqrt(st[0:c, 1:2], st[0:c, 1:2], EPS, r=[stk], w=[stk], tmp=st[0:c, 3:4], tmpk=stk)
                self.TS(vpre[0:c], vpre[0:c], st[0:c, 0:1], st[0:c, 1:2], ALU.subtract, ALU.mult, r=[vpk, stk], w=[vpk])
                self.TT(vpre[0:c], vpre[0:c], self.pprow[0:c, 0:512], ALU.mult, r=[vpk, "pprow"], w=[vpk])
                self.TT(vpre[0:c], vpre[0:c], self.pprow[0:c, 512:1024], ALU.add, r=[vpk, "pprow"], w=[vpk])
                if self.kind == "s":
                    self.DMA("sp", self.o_sgv[l, si], vpre[0:c], r=[vpk], w=[])
                self.CP(vnb[0:c], vpre[0:c], r=[vpk], w=[vbk], eng="act")
                pb2, pk2 = self.ps()
                for h in range(4):
                    wT = self.sgwb[:, l, h, :] if self.kind == "p" else self.sgwsb[:, l, h, :]
                    self.MM(pb2[:, h * 128:h * 128 + c], vnb[0:c, h * 128:(h + 1) * 128], wT, True, True,
                            r=[vbk, "sgwb", "sgwsb"], w=[pk2])
                p2v = pb2[:].rearrange("p (h i) -> p h i", h=4)
                brow = self.pprow[:, 1024:1536].rearrange("p (h i) -> p h i", h=4)
                self.TT(sT[:, :, a:a + c], p2v[:, :, 0:c], brow[:, :, 0:c], ALU.add, r=[pk2, "pprow"], w=[sTk])
        for c in range(4):
            pb, pk = self.proj()
            self.CP(yy, pb[:, 0:T], r=[pk], w=[yk], eng="act")
            self.gelu(yy, yy, tg, r=[yk], w=[yk], tk=tgk)
            self.TT(self.mg[:, 8 + c, 0:T], yy, sT[:, c, :], ALU.mult, r=[yk, sTk], w=["o:mg%d" % (8 + c)])

    def grpD(self, l):
        T = self.T
        self.mt_reset()
        nseg = len(self.segs)
        Lseg = self.segs[0][1]
        cd = 64 if self.kind == "p" else 32
        nlev = 5 if cd == 64 else 4
        QT, QTk = self.mt([128, 4, T], BF16)
        KT, KTk = self.mt([128, 4, T], BF16)
        VT, VTk = self.mt([128, 4, T], BF16)
        grow, grk = self.mt([4, T], F32)
        gcrow, gck = self.mt([4, T], F32)
        brow, brk = self.mt([4, T], F32)
        mark = self._mt
        slot = self.pj_next()
        pa, pak = self.ps()
        for kc in range(KC):
            self.MM(pa[0:4, 0:T], self.wring[slot][:, kc, 0:4], self.xbf[:, kc, 0:T], kc == 0, kc == KC - 1,
                    r=["wr%d" % slot, "xbf%d" % kc], w=[pak])
        pbb, pbk = self.ps()
        for kc in range(KC):
            self.MM(pbb[0:4, 0:T], self.wring[slot][:, kc, 4:8], self.xbf[:, kc, 0:T], kc == 0, kc == KC - 1,
                    r=["wr%d" % slot, "xbf%d" % kc], w=[pbk])
        self.ACT(grow, pa[0:4, 0:T], AF.Exp, r=[pak, "dn4"], w=[grk], bias=self.dn4[:, l, 1:2])
        self.ACT(grow, grow, AF.Ln, r=[grk], w=[grk], bias=1.0)
        self.TS(grow, grow, self.dn4[:, l, 2:3], None, ALU.mult, None, r=[grk, "dn4"], w=[grk])
        self.ACT(brow, pbb[0:4, 0:T], AF.Sigmoid, r=[pbk], w=[brk])
        for a in range(0, T, cd):
            self.SCAN(gcrow[:, a:a + cd], self.ones4[0:4, 0:cd], grow[:, a:a + cd], 0.0, r=[grk, "ones4"], w=[gck])
        xpre, xpk = self.mt([128, nseg, Lseg + 3], F32)
        y32, y3k = self.mt([128, T], F32)
        sqb, sqk = self.mt([128, T], BF16)
        rn, rnk = self.mt([128, T], F32)
        rt_, rtk = self.mt([128, T], F32)
        for c in range(12):
            pb, pk = self.proj()
            for si, (c0, L, slot_) in enumerate(self.segs):
                self.CP(xpre[:, si, 0:3], self.dct[:, (l * 2 + slot_) * 12 + c, :], r=["dct"], w=[xpk])
                self.CP(xpre[:, si, 3:3 + L], pb[:, c0:c0 + L], r=[pk], w=[xpk], eng="act")
                w0 = 128 + c * 4
                self.TS(y32[:, c0:c0 + L], xpre[:, si, 0:L], self.ppfm[:, l, w0:w0 + 1], None, ALU.mult, None,
                        r=[xpk, "ppfm"], w=[y3k])
                for j in range(1, 4):
                    self.STT(y32[:, c0:c0 + L], xpre[:, si, j:j + L], self.ppfm[:, l, w0 + j:w0 + j + 1], y32[:, c0:c0 + L],
                             ALU.mult, ALU.add, r=[xpk, "ppfm", y3k], w=[y3k])
                self.CP(self.dct[:, (l * 2 + slot_) * 12 + c, :], xpre[:, si, L:L + 3], r=[xpk], w=["dct"])
            self.ACT(y32, y32, AF.Silu, r=[y3k], w=[y3k])
            h = c % 4
            if c < 8:
                self.ACT(sqb, y32, AF.Square, r=[y3k], w=[sqk])
                pq, pqk = self.ps()
                self.MM(pq[:, 0:T], self.onesb[:], sqb, True, True, r=["onesb", sqk], w=[pqk])
                self.rsqrt(rn, pq[:, 0:T], 1e-6, r=[pqk], w=[rnk], tmp=rt_, tmpk=rtk)
                if c < 4:
                    self.STT(QT[:, h, :], y32, 128 ** -0.5, rn, ALU.mult, ALU.mult, r=[y3k, rnk], w=[QTk])
                else:
                    self.TT(KT[:, h, :], y32, rn, ALU.mult, r=[y3k, rnk], w=[KTk])
            else:
                self.CP(VT[:, h, :], y32, r=[y3k], w=[VTk], eng="act")
        def dump(i):
            if self.cfg.get("dbg") and self.kind == "p" and l == 0 and self.ti == 0:
                for j, (t_, k_) in enumerate(((QT, QTk), (KT, KTk), (VT, VTk))):
                    self.DMA("sp", self.dbg_q[i][:, j * 4:(j + 1) * 4, :], t_, r=[k_], w=[])
        dump(0)
        self.barrier()
        self._mt = mark
        H = 4
        colq, cqk = self.mt([64, 16], F32)
        kdc, kdk = self.mt([64, 4], F32)
        egl, eglk = self.mt([128, 4], F32)
        gcB, gBk = self.mt([128, 4, 64], F32)
        egB, eBk = self.mt([128, 4, 64], F32)
        msk, mkk = self.mt([4, 4, 64], F32)
        Dm, Dmk = self.mt([64, 4, 64], F32)
        DTm, DTk = self.mt([64, 4, 64], F32)
        Nm = [self.mt([64, 4, 64], F32) for _ in range(2)]
        NTm = [self.mt([64, 4, 64], F32) for _ in range(2)]
        XT, XTk = self.mt([64, 4, 64], F32)
        XTb, XTbk = self.mt([64, 4, 64], BF16)
        atb, atk = self.mt([64, 4, 64], BF16)
        bV, bVk = self.mt([64, 4, 128], BF16)
        Kbg, Kbk = self.mt([64, 4, 128], BF16)
        Kd, Kdk = self.mt([64, 4, 128], BF16)
        nwT, nwk = self.mt([128, 4, 64], BF16)
        QgT, Qgk = self.mt([128, 4, 64], BF16)
        vnew, vnk = self.mt([64, 4, 128], BF16)
        o32, o3k = self.mt([64, 4, 128], F32)
        osq, oqk = self.mt([64, 4, 128], F32)
        st, stk = self.mt([64, 12], F32)
        Sbf, Sbk = self.mt([128, 4, 128], BF16)
        Lm = self.mask[0:cd, 1, 0:cd]
        Um = self.mask[0:cd, 2, 0:cd]
        NCH = self.cfg.get("dn_chains", 2)
        HPC = 4 // NCH

        def chain(ch, a, c, S32, first):
            hs = list(range(ch * HPC, (ch + 1) * HPC))
            h0, h1 = hs[0], hs[-1] + 1
            nh = len(hs)
            sfx = "_c%d" % ch

            def K(k):
                return k + sfx
            SK = "Sdn" + sfx
            pc, pck = self.ps()
            self.MM(pc[0:c, 0:nh], gcrow[0:4, a:a + c], self.ident32[0:4, h0:h1], True, True, r=[gck, "ident32"], w=[pck])
            self.MM(pc[0:c, 4:4 + nh], brow[0:4, a:a + c], self.ident32[0:4, h0:h1], True, True, r=[brk, "ident32"], w=[pck])
            pbc, pbck = self.ps()
            for h in hs:
                self.TS(msk[:, h, 0:c], gcrow[0:4, a:a + c], self.ident32[0:4, h:h + 1], None, ALU.mult, None,
                        r=[gck, "ident32"], w=[K(mkk)])
                self.MM(pbc[:, (h - h0) * 64:(h - h0) * 64 + c], self.ones4[:], msk[:, h, 0:c], True, True,
                        r=["ones4", K(mkk)], w=[pbck])
            yield
            self.CP(colq[0:c, h0:h1], pc[0:c, 0:nh], r=[pck], w=[K(cqk)])
            self.CP(colq[0:c, 4 + h0:4 + h1], pc[0:c, 4:4 + nh], r=[pck], w=[K(cqk)])
            self.TS(colq[0:c, 8 + h0:8 + h1], colq[0:c, 4 + h0:4 + h1], -1.0, None, ALU.mult, None, r=[K(cqk)], w=[K(cqk)])
            self.ACT(colq[0:c, 12 + h0:12 + h1], colq[0:c, h0:h1], AF.Exp, r=[K(cqk)], w=[K(cqk)])
            pbv = pbc[:, 0:nh * 64].rearrange("p (h i) -> p h i", h=nh)
            self.CP(gcB[:, h0:h1, 0:c], pbv[:, :, 0:c], r=[pbck], w=[K(gBk)])
            yield
            self.ACT(egB[:, h0:h1, 0:c], gcB[:, h0:h1, 0:c], AF.Exp, r=[K(gBk)], w=[K(eBk)])
            self.TT(kdc[0:c, h0:h1], gcB[0:c, h0:h1, c - 1], colq[0:c, h0:h1], ALU.subtract, r=[K(gBk), K(cqk)], w=[K(kdk)])
            for h in hs:
                self.TS(Dm[0:c, h, 0:c], gcB[0:c, h, 0:c], colq[0:c, h:h + 1], -1.0, ALU.subtract, ALU.mult,
                        r=[K(gBk), K(cqk)], w=[K(Dmk)])
                self.TS(DTm[0:c, h, 0:c], gcB[0:c, h, 0:c], colq[0:c, h:h + 1], 0.0, ALU.subtract, ALU.min,
                        r=[K(gBk), K(cqk)], w=[K(DTk)])
            self.TS(Dm[0:c, h0:h1, 0:c], Dm[0:c, h0:h1, 0:c], 0.0, None, ALU.min, None, r=[K(Dmk)], w=[K(Dmk)])
            pG, pGk = self.ps()
            for h in hs:
                hi = h - h0
                self.MM(pG[0:c, hi * 64:hi * 64 + c], KT[:, h, a:a + c], KT[:, h, a:a + c], True, True, r=[KTk], w=[pGk])
                self.MM(pG[0:c, 256 + hi * 64:256 + hi * 64 + c], KT[:, h, a:a + c], QT[:, h, a:a + c], True, True,
                        r=[KTk, QTk], w=[pGk])
            yield
            self.CP(egl[:, h0:h1], egB[:, h0:h1, c - 1], r=[K(eBk)], w=[K(eglk)])
            self.ACT(kdc[0:c, h0:h1], kdc[0:c, h0:h1], AF.Exp, r=[K(kdk)], w=[K(kdk)])
            self.ACT(Dm[0:c, h0:h1, 0:c], Dm[0:c, h0:h1, 0:c], AF.Exp, r=[K(Dmk)], w=[K(Dmk)])
            self.ACT(DTm[0:c, h0:h1, 0:c], DTm[0:c, h0:h1, 0:c], AF.Exp, r=[K(DTk)], w=[K(DTk)])
            yield
            for h in hs:
                self.TT(Dm[0:c, h, 0:c], Dm[0:c, h, 0:c], Lm, ALU.mult, r=[K(Dmk), "mask"], w=[K(Dmk)])
                self.TT(DTm[0:c, h, 0:c], DTm[0:c, h, 0:c], Um, ALU.mult, r=[K(DTk), "mask"], w=[K(DTk)])
            (N0, N0k), (N1, N1k) = Nm
            (NT0, NT0k), (NT1, NT1k) = NTm
            for h in hs:
                hi = h - h0
                self.STT(N0[0:c, h, 0:c], pG[0:c, hi * 64:hi * 64 + c], colq[0:c, 8 + h:9 + h], Dm[0:c, h, 0:c],
                         ALU.mult, ALU.mult, r=[pGk, K(cqk), K(Dmk)], w=[K(N0k)])
            pGv = pG[:, 256:256 + nh * 64].rearrange("p (h i) -> p h i", h=nh)
            self.TT(atb[0:c, h0:h1, 0:c], pGv[0:c, :, 0:c], DTm[0:c, h0:h1, 0:c], ALU.mult, r=[pGk, K(DTk)], w=[K(atk)])
            pN, pNk = self.ps()
            for h in hs:
                hi = h - h0
                self.MM(pN[0:c, hi * 64:hi * 64 + c], N0[0:c, h, 0:c], self.ident32[0:c, 0:c], True, True,
                        r=[K(N0k), "ident32"], w=[pNk])
            yield
            pNv = pN[:, 0:nh * 64].rearrange("p (h i) -> p h i", h=nh)
            self.CP(NT0[0:c, h0:h1, 0:c], pNv[0:c, :, 0:c], r=[pNk], w=[K(NT0k)], eng="act")
            yield
            for h in hs:
                self.TT(XT[0:c, h, 0:c], NT0[0:c, h, 0:c], self.ident32[0:c, 0:c], ALU.add, r=[K(NT0k), "ident32"], w=[K(XTk)])
            Pc, Pck, PTc, PTck = N0, N0k, NT0, NT0k
            Pn, Pnk, PTn, PTnk = N1, N1k, NT1, NT1k
            for lev in range(nlev):
                pP, pPk = self.ps()
                for h in hs:
                    hi = h - h0
                    self.MM(pP[0:c, hi * 64:hi * 64 + c], PTc[0:c, h, 0:c], Pc[0:c, h, 0:c], True, True, r=[K(PTck), K(Pck)], w=[pPk])
                    if lev < nlev - 1:
                        self.MM(pP[0:c, 256 + hi * 64:256 + hi * 64 + c], Pc[0:c, h, 0:c], PTc[0:c, h, 0:c], True, True,
                                r=[K(PTck), K(Pck)], w=[pPk])
                yield
                pPa = pP[:, 0:nh * 64].rearrange("p (h i) -> p h i", h=nh)
                pPb = pP[:, 256:256 + nh * 64].rearrange("p (h i) -> p h i", h=nh)
                self.CP(Pn[0:c, h0:h1, 0:c], pPa[0:c, :, 0:c], r=[pPk], w=[K(Pnk)])
                if lev < nlev - 1:
                    self.CP(PTn[0:c, h0:h1, 0:c], pPb[0:c, :, 0:c], r=[pPk], w=[K(PTnk)], eng="act")
                yield
                pX, pXk = self.ps()
                for h in hs:
                    hi = h - h0
                    self.MM(pX[0:c, hi * 64:hi * 64 + c], Pn[0:c, h, 0:c], XT[0:c, h, 0:c], True, True, r=[K(Pnk), K(XTk)], w=[pXk])
                yield
                pXv = pX[:, 0:nh * 64].rearrange("p (h i) -> p h i", h=nh)
                self.TT(XT[0:c, h0:h1, 0:c], XT[0:c, h0:h1, 0:c], pXv[0:c, :, 0:c], ALU.add, r=[K(XTk), pXk], w=[K(XTk)])
                Pc, Pck, PTc, PTck, Pn, Pnk, PTn, PTnk = Pn, Pnk, PTn, PTnk, Pc, Pck, PTc, PTck
            pK, pKk = self.ps()
            for h in hs:
                hi = h - h0
                self.MM(pK[0:c, hi * 128:(hi + 1) * 128], KT[:, h, a:a + c], self.identb[:], True, True, r=[KTk, "identb"], w=[pKk])
                self.MM(pK[0:c, 256 + hi * 128:256 + (hi + 1) * 128], VT[:, h, a:a + c], self.identb[:], True, True,
                        r=[VTk, "identb"], w=[pKk])
            yield
            self.CP(XTb[0:c, h0:h1, 0:c], XT[0:c, h0:h1, 0:c], r=[K(XTk)], w=[K(XTbk)], eng="act")
            for h in hs:
                hi = h - h0
                self.TS(Kbg[0:c, h, :], pK[0:c, hi * 128:(hi + 1) * 128], colq[0:c, 4 + h:5 + h], colq[0:c, 12 + h:13 + h],
                        ALU.mult, ALU.mult, r=[pKk, K(cqk)], w=[K(Kbk)])
                self.TS(Kd[0:c, h, :], pK[0:c, hi * 128:(hi + 1) * 128], kdc[0:c, h:h + 1], None, ALU.mult, None,
                        r=[pKk, K(kdk)], w=[K(Kdk)])
                self.TS(bV[0:c, h, :], pK[0:c, 256 + hi * 128:256 + (hi + 1) * 128], colq[0:c, 4 + h:5 + h], None, ALU.mult, None,
                        r=[pKk, K(cqk)], w=[K(bVk)])
            self.TT(QgT[:, h0:h1, 0:c], QT[:, h0:h1, a:a + c], egB[:, h0:h1, 0:c], ALU.mult, r=[QTk, K(eBk)], w=[K(Qgk)])
            yield
            pW, pWk = self.ps()
            for h in hs:
                hi = h - h0
                self.MM(pW[:, hi * 64:hi * 64 + c], Kbg[0:c, h, :], XTb[0:c, h, 0:c], True, True, r=[K(Kbk), K(XTbk)], w=[pWk])
            yield
            pWv = pW[:, 0:nh * 64].rearrange("p (h i) -> p h i", h=nh)
            self.ACT(nwT[:, h0:h1, 0:c], pWv[:, :, 0:c], AF.Copy, r=[pWk], w=[K(nwk)], scale=-1.0)
            yield
            if first:
                self.CP(Sbf[:, h0:h1, :], S32[:, h0:h1, :], r=["Sdn", SK], w=[K(Sbk)], eng="act")
                yield
            pU, pUk = self.ps()
            for h in hs:
                hi = h - h0
                self.MM(pU[0:c, hi * 128:(hi + 1) * 128], XTb[0:c, h, 0:c], bV[0:c, h, :], True, False, r=[K(XTbk), K(bVk)], w=[pUk])
                self.MM(pU[0:c, hi * 128:(hi + 1) * 128], nwT[:, h, 0:c], Sbf[:, h, :], False, True, r=[K(nwk), K(Sbk)], w=[pUk])
            yield
            pUv = pU[:, 0:nh * 128].rearrange("p (h e) -> p h e", h=nh)
            self.CP(vnew[0:c, h0:h1, :], pUv[0:c], r=[pUk], w=[K(vnk)], eng="act")
            yield
            pO, pOk = self.ps()
            for h in hs:
                hi = h - h0
                self.MM(pO[0:c, hi * 128:(hi + 1) * 128], QgT[:, h, 0:c], Sbf[:, h, :], True, False, r=[K(Qgk), K(Sbk)], w=[pOk])
                self.MM(pO[0:c, hi * 128:(hi + 1) * 128], atb[0:c, h, 0:c], vnew[0:c, h, :], False, True, r=[K(atk), K(vnk)], w=[pOk])
                self.MM(pO[:, 256 + hi * 128:256 + (hi + 1) * 128], Kd[0:c, h, :], vnew[0:c, h, :], True, True, r=[K(Kdk), K(vnk)], w=[pOk])
            yield
            pOv = pO[:, 0:nh * 128].rearrange("p (h e) -> p h e", h=nh)
            self.CP(o32[0:c, h0:h1, :], pOv[0:c], r=[pOk], w=[K(o3k)], eng="act")
            for h in hs:
                hi = h - h0
                self.STT(S32[:, h, :], S32[:, h, :], egl[:, h:h + 1], pO[:, 256 + hi * 128:256 + (hi + 1) * 128], ALU.mult, ALU.add,
                         r=["Sdn", SK, K(eglk), pOk], w=[SK])
            yield
            self.CP(Sbf[:, h0:h1, :], S32[:, h0:h1, :], r=[SK], w=[K(Sbk)], eng="act")
            self.TT(osq[0:c, h0:h1, :], o32[0:c, h0:h1, :], o32[0:c, h0:h1, :], ALU.mult, r=[K(o3k)], w=[K(oqk)])
            self.RSUM(st[0:c, h0:h1], osq[0:c, h0:h1, :], r=[K(oqk)], w=[K(stk)])
            self.rsqrt(st[0:c, 4 + h0:4 + h1], st[0:c, h0:h1], EPS, r=[K(stk)], w=[K(stk)], tmp=st[0:c, 8 + h0:8 + h1], tmpk=K(stk),
                       scale=1.0 / 128)
            for h in hs:
                self.TS(o32[0:c, h, :], o32[0:c, h, :], st[0:c, 4 + h:5 + h], None, ALU.mult, None, r=[K(o3k), K(stk)], w=[K(o3k)])
            yield
            pT, pTk = self.ps()
            for h in hs:
                hi = h - h0
                self.MM(pT[:, hi * 64:hi * 64 + c], o32[0:c, h, :], self.ident32[0:c, 0:c], True, True, r=[K(o3k), "ident32"], w=[pTk])
            yield
            for h in hs:
                hi = h - h0
                self.TS(self.mg[:, 12 + h, a:a + c], pT[:, hi * 64:hi * 64 + c], self.ppfm[:, l, 180 + h:181 + h], None,
                        ALU.mult, None, r=[pTk, "ppfm"], w=["o:mg%d" % (12 + h)])

        import itertools
        for (c0, L, slot_) in self.segs:
            S32 = self.Sdn[:, l * 2 + slot_, :].rearrange("p (h e) -> p h e", h=4)
            for ci in range(L // cd):
                a = c0 + ci * cd
                gens = [chain(ch, a, cd, S32, ci == 0) for ch in range(NCH)]
                for _ in itertools.zip_longest(*gens):
                    pass
            self.CP(self.bar[:, 0:1], self.bar[:, 0:1], r=["Sdn_c%d" % ch for ch in range(NCH)] + ["bar"], w=["Sdn", "bar"])
        dump(1)
        zz, zk = self.mt([128, T], F32)
        for h in range(4):
            pb, pk = self.proj()
            self.ACT(zz, pb[:, 0:T], AF.Silu, r=[pk], w=[zk])
            self.TT(self.mg[:, 12 + h, 0:T], self.mg[:, 12 + h, 0:T], zz, ALU.mult, r=[zk, "o:mg%d" % (12 + h)],
                    w=["o:mg%d" % (12 + h)])


def host_pack(inp):
    fm = np.zeros((NL, 128, NFM), np.float32)
    row = np.zeros((NL, 128, NROW), np.float32)
    bd = np.zeros((NL, 128, 2, 4, 128), np.float32)
    sgw = np.zeros((NL, 128, 4, 128), np.float32)
    sgws = np.zeros((NL, 32, 4, 32), np.float32)
    dn4 = np.zeros((NL, 4, 2), np.float32)

    def colmaj(v, n):
        return v.reshape(n, 128).T

    for l in range(NL):
        o = 0
        for nm in ["ln1_g", "ln1_b", "ln2_g", "ln2_b", "ln3_g", "ln3_b"]:
            fm[l, :, o:o + 16] = colmaj(inp[nm][l], 16)
            o += 16
        cw = inp["lru_conv_w"][l]
        for c in range(4):
            for j in range(4):
                fm[l, :, 96 + c * 4 + j] = cw[j, c * 128:(c + 1) * 128]
        fm[l, :, 112:116] = colmaj(inp["lru_conv_b"][l], 4)
        fm[l, :, 116:120] = colmaj(inp["lru_b_a"][l], 4)
        fm[l, :, 120:124] = colmaj(inp["lru_b_x"][l], 4)
        fm[l, :, 124:128] = colmaj(inp["lru_lam"][l], 4)
        dw = inp["dn_conv_w"][l]
        for c in range(12):
            for j in range(4):
                fm[l, :, 128 + c * 4 + j] = dw[j, c * 128:(c + 1) * 128]
        fm[l, :, 176:180] = colmaj(inp["ret_norm_g"][l], 4)
        for h in range(4):
            fm[l, :, 180 + h] = inp["dn_norm_g"][l]
        row[l, :, 0:512] = inp["sg_ln_g"][l][None, :]
        row[l, :, 512:1024] = inp["sg_ln_b"][l][None, :]
        row[l, :, 1024:1536] = inp["sg_b"][l].reshape(1, 512)
        for t, nm in enumerate(["lru_w_a", "lru_w_x"]):
            w = inp[nm][l]
            for c in range(4):
                bd[l, 0:64, t, c, 0:64] = w[2 * c]
                bd[l, 64:128, t, c, 64:128] = w[2 * c + 1]
        sgw[l] = inp["sg_w"][l].transpose(2, 0, 1)
        sgws[l] = inp["sg_w"][l][:, :32, :32].transpose(2, 0, 1)
        dn4[l, :, 0] = inp["dn_a_log"][l]
        dn4[l, :, 1] = inp["dn_dt_bias"][l]
    return dict(pp_fm=fm, pp_row=row, pp_bd=np.ascontiguousarray(bd.reshape(NL, 128, 8, 128)), pp_sgw=sgw, pp_sgws=sgws, pp_dn4=dn4)


def make_in_maps(inp, cfg):
    nt = cfg.get("ntiles", 4)
    LP = max(nt, 1) * 512
    hc = host_consts()
    pk = host_pack(inp)
    maps = []
    for c in range(8):
        m = {}
        m["xp"] = np.ascontiguousarray(inp["x_prompt"][c % 4, :LP])
        m["xs"] = np.ascontiguousarray(inp["x_sample"][2 * c:2 * c + 2].reshape(64, D))
        m["w1i"] = inp["ffn1_w_in"]
        m["w1o"] = inp["ffn1_w_out"]
        m["wmi"] = inp["w_mix_in"]
        m["wmo"] = inp["w_mix_out"]
        m["w2i"] = inp["ffn2_w_in"]
        m["w2o"] = inp["ffn2_w_out"]
        m.update(pk)
        for k in ["c_ident", "c_cos", "c_sin", "c_rt", "c_kd", "c_mask"]:
            m[k] = hc[k]
        sl = slice(2 * c, 2 * c + 2)
        m["st_ret"] = np.ascontiguousarray(inp["state_ret"][:, sl])
        m["st_dn"] = np.ascontiguousarray(inp["state_dn"][:, sl])
        m["st_lruh"] = np.ascontiguousarray(inp["state_lru_h"][:, sl].reshape(NL, 2, 4, 128).transpose(0, 1, 3, 2))
        m["st_lruconv"] = np.ascontiguousarray(inp["state_lru_conv"][:, sl].reshape(NL, 2, 3, 4, 128).transpose(0, 1, 4, 3, 2))
        m["st_dnconv"] = np.ascontiguousarray(inp["state_dn_conv"][:, sl].reshape(NL, 2, 3, 12, 128).transpose(0, 1, 4, 3, 2))
        maps.append(m)
    return maps


def assemble(results):
    yp = np.stack([results[c]["yp"] for c in range(4)])
    ys = np.concatenate([results[c]["ys"].reshape(2, 32, D) for c in range(8)], 0)

    def st(name, o):
        return np.stack([results[c][name][:, o] for c in range(4)], 1)

    def ss(name):
        return np.concatenate([results[c][name][:, 1:3] for c in range(8)], 1)

    def lruh(a):
        return np.ascontiguousarray(a.transpose(0, 1, 3, 2).reshape(a.shape[0], a.shape[1], 512))

    def conv(a, n):
        return np.ascontiguousarray(a.transpose(0, 1, 4, 3, 2).reshape(a.shape[0], a.shape[1], 3, n * 128))

    ret_p, dn_p = st("o_ret", 0), st("o_dn", 0)
    lruh_p = lruh(st("o_lruh", 0))
    lruc_p = conv(st("o_lruconv", 0), 4)
    dnc_p = conv(st("o_dnconv", 0), 12)
    ret_s, dn_s = ss("o_ret"), ss("o_dn")
    lruh_s = lruh(ss("o_lruh"))
    lruc_s = conv(ss("o_lruconv"), 4)
    dnc_s = conv(ss("o_dnconv"), 12)
    sgv = np.concatenate([results[c]["o_sgv"] for c in range(8)], 1)
    outs = (yp, ys, ret_p, lruh_p, lruc_p, dn_p, dnc_p, ret_s, lruh_s, lruc_s, dn_s, dnc_s, sgv)
    return tuple(np.ascontiguousarray(o, dtype=np.float32) for o in outs)


_CACHE = {}


def kernel(**inputs):
    inp = {k: np.asarray(v) for k, v in inputs.items()}
    cfg = dict(ntiles=4, depth=NL)
    if "nc" not in _CACHE:
        _CACHE["nc"] = Builder(cfg).build()
    nc = _CACHE["nc"]
    maps = make_in_maps(inp, cfg)
    res = run_bass_kernel_spmd(nc, maps, core_ids=list(range(8)))
    return assemble(res.results)
```
